# Optimizing a Trainium2 kernel written in Bass

```python
import math
import jax, jax.numpy as jnp
from jax import lax
import numpy as np

D_MODEL = 1024
BATCH = 8
SEQ = 2048
DEPTH = 2

CTX_LEN = 256
GRID_W = 64
EPS = 1e-6

MLA_HEADS = 8
MLA_NOPE = 64
MLA_ROPE = 32
MLA_V = 64
MLA_Q_RANK = 256
MLA_KV_RANK = 128
ROPE_BASE = 10000.0
Q_BLOCK = 128
ATTN_SCALE = (MLA_NOPE + MLA_ROPE) ** -0.5

SSM_HEADS = 8
SSM_HEADDIM = 64
SSM_INNER = SSM_HEADS * SSM_HEADDIM
SSM_GROUPS = 2
SSM_STATE = 128
SSM_CONV = 5
SSM_CHUNK = 128
XBC_DIM = SSM_INNER + 2 * SSM_GROUPS * SSM_STATE

GM_GROUPS = 8
GM_WIDTH = 512
GM_GDIM = GM_WIDTH // GM_GROUPS
GM_CHUNK = 128

D_FF = 2816
N_BRANCH = 3
N_MOD = 9

KV_SIZES = (MLA_KV_RANK, MLA_ROPE, XBC_DIM, SSM_HEADS, SSM_HEADS)
Q_SIZES = (MLA_Q_RANK, SSM_INNER, 2 * GM_WIDTH, N_BRANCH * D_MODEL)
KV_SIDE = MLA_KV_RANK + MLA_ROPE + XBC_DIM + 2 * SSM_HEADS
D_IN = KV_SIDE + MLA_Q_RANK + SSM_INNER + 2 * GM_WIDTH + N_BRANCH * D_MODEL

kernel_name = "hybrid_mla_ssd_gmlp_prefix_dit"


def _split(t, sizes):
    offs = []
    acc = 0
    for s in sizes[:-1]:
        acc += s
        offs.append(acc)
    return jnp.split(t, offs, axis=-1)


def rmsnorm(x, g):
    xf = x.astype(jnp.float32)
    y = xf * lax.rsqrt(jnp.mean(xf * xf, axis=-1, keepdims=True) + EPS)
    return (y * g.astype(jnp.float32)).astype(x.dtype)


def layernorm(x, g):
    xf = x.astype(jnp.float32)
    xf = xf - jnp.mean(xf, axis=-1, keepdims=True)
    y = xf * lax.rsqrt(jnp.mean(xf * xf, axis=-1, keepdims=True) + EPS)
    return (y * g.astype(jnp.float32)).astype(x.dtype)


def pre(h, g, shift, scale):
    return rmsnorm(h, g) * (1.0 + scale) + shift


def swiglu(h, w_i, w_o):
    gate, up = jnp.split(h @ w_i, 2, axis=-1)
    return (jax.nn.silu(gate) * up) @ w_o


def axial_rope_tables(rows):
    row = jnp.repeat(jnp.arange(rows, dtype=jnp.float32), GRID_W)
    col = jnp.tile(jnp.arange(GRID_W, dtype=jnp.float32), rows)
    n_freq = MLA_ROPE // 4
    inv = jnp.power(ROPE_BASE, -jnp.arange(n_freq, dtype=jnp.float32) / n_freq)
    ang = jnp.concatenate([row[:, None] * inv, col[:, None] * inv], axis=-1)
    return jnp.cos(ang), jnp.sin(ang)


def apply_rope(x, cos, sin):
    half = x.shape[-1] // 2
    x1, x2 = x[..., :half], x[..., half:]
    return jnp.concatenate([x1 * cos - x2 * sin, x1 * sin + x2 * cos], axis=-1).astype(x.dtype)


def mla_keys(kv_lat, p):
    B, L, _ = kv_lat.shape
    kv = (rmsnorm(kv_lat, p["mla_kv_norm"]) @ p["mla_w_ukv"]).reshape(B, L, MLA_HEADS, MLA_NOPE + MLA_V)
    return kv[..., :MLA_NOPE], kv[..., MLA_NOPE:]


def mla_queries(q_lat, p):
    B, L, _ = q_lat.shape
    q = (rmsnorm(q_lat, p["mla_q_norm"]) @ p["mla_w_uq"]).reshape(B, L, MLA_HEADS, MLA_NOPE + MLA_ROPE)
    return q[..., :MLA_NOPE], q[..., MLA_NOPE:]


def block_attention(q_nope, q_rope, k_nope, k_rope, v):
    B, Lq, H, _ = q_nope.shape
    nb = Lq // Q_BLOCK

    def blocks(t):
        return jnp.moveaxis(t.reshape((B, nb, Q_BLOCK) + t.shape[2:]), 1, 0)

    def one(qs):
        qn, qr = qs
        s = jnp.einsum('bqhd,bkhd->bhqk', qn, k_nope) + jnp.einsum('bqhr,bkr->bhqk', qr, k_rope)
        pr = jax.nn.softmax(s.astype(jnp.float32) * ATTN_SCALE, axis=-1).astype(v.dtype)
        return jnp.einsum('bhqk,bkhd->bqhd', pr, v)

    o = lax.map(one, (blocks(q_nope), blocks(q_rope)))
    return jnp.moveaxis(o, 0, 1).reshape(B, Lq, H * MLA_V)


def dwconv_centred(x, w, b):
    pad = SSM_CONV // 2
    out = lax.conv_general_dilated(x, w.T[:, None, :].astype(x.dtype), window_strides=(1,),
                                   padding=[(pad, pad)], dimension_numbers=('NWC', 'WIO', 'NWC'),
                                   feature_group_count=x.shape[-1])
    return out + b


def ssm_prep(xbc, dtf, dtb, p):
    B, L, _ = xbc.shape
    xbc = jax.nn.silu(dwconv_centred(xbc, p["ssm_conv_w"], p["ssm_conv_b"]))
    xs, Bm, Cm = _split(xbc, (SSM_INNER, SSM_GROUPS * SSM_STATE, SSM_GROUPS * SSM_STATE))
    xs = xs.reshape(B, L, SSM_HEADS, SSM_HEADDIM)
    Bm = Bm.reshape(B, L, SSM_GROUPS, SSM_STATE)
    Cm = Cm.reshape(B, L, SSM_GROUPS, SSM_STATE)
    dt_bias = p["ssm_dt_bias"].astype(jnp.float32)
    dt_f = jax.nn.softplus(dtf.astype(jnp.float32) + dt_bias[0])
    dt_b = jax.nn.softplus(dtb.astype(jnp.float32) + dt_bias[1])
    return xs, Bm, Cm, dt_f, dt_b


def ssd(x, dt, A, Bm, Cm, h0, need_y):
    Bsz, L, H, P = x.shape
    N = Bm.shape[-1]
    Q = SSM_CHUNK
    nc = L // Q
    rep = H // Bm.shape[2]
    Bc = jnp.repeat(Bm, rep, axis=2).astype(jnp.float32).reshape(Bsz, nc, Q, H, N)
    Cc = jnp.repeat(Cm, rep, axis=2).astype(jnp.float32).reshape(Bsz, nc, Q, H, N)
    xs = x.astype(jnp.float32).reshape(Bsz, nc, Q, H, P)
    dtc = dt.reshape(Bsz, nc, Q, H)
    a_cum = jnp.cumsum(dtc * A, axis=2)
    a_tot = a_cum[:, :, -1]
    w_state = jnp.exp(a_tot[:, :, None] - a_cum) * dtc
    S = jnp.einsum('bcqh,bcqhn,bcqhp->bchpn', w_state, Bc, xs)

    def step(h, inp):
        decay, s = inp
        return decay[:, :, None, None] * h + s, h

    h_last, h_prev = lax.scan(step, h0, (jnp.moveaxis(jnp.exp(a_tot), 1, 0), jnp.moveaxis(S, 1, 0)))
    if not need_y:
        return None, h_last
    h_prev = jnp.moveaxis(h_prev, 0, 1)
    y_inter = jnp.einsum('bcqhn,bchpn->bcqhp', Cc, h_prev) * jnp.exp(a_cum)[..., None]
    mask = jnp.tril(jnp.ones((Q, Q), dtype=bool))
    diff = a_cum[:, :, :, None, :] - a_cum[:, :, None, :, :]
    decay_mat = jnp.exp(jnp.where(mask[None, None, :, :, None], diff, -jnp.inf))
    scores = jnp.einsum('bcihn,bcjhn->bcijh', Cc, Bc) * decay_mat * dtc[:, :, None, :, :]
    y_intra = jnp.einsum('bcijh,bcjhp->bcihp', scores, xs)
    y = (y_intra + y_inter).reshape(Bsz, L, H, P)
    return y.astype(x.dtype), h_last


def ssm_bidir(lat, ctx, p, with_ctx):
    xl, Bl, Cl, dfl, dbl = lat
    xc, Bc, Cc, dfc, dbc = ctx
    A = -jnp.exp(p["ssm_a_log"].astype(jnp.float32))
    Bsz = xl.shape[0]
    h0 = jnp.zeros((Bsz, SSM_HEADS, SSM_HEADDIM, SSM_STATE), jnp.float32)
    fl = lambda t: jnp.flip(t, axis=1)
    yc_f, hc_f = ssd(xc, dfc, A[0], Bc, Cc, h0, with_ctx)
    yl_f, _ = ssd(xl, dfl, A[0], Bl, Cl, hc_f, True)
    yc_b, hc_b = ssd(fl(xc), fl(dbc), A[1], fl(Bc), fl(Cc), h0, with_ctx)
    yl_b, _ = ssd(fl(xl), fl(dbl), A[1], fl(Bl), fl(Cl), hc_b, True)
    d = p["ssm_d"][None, None, :, None]
    y_l = yl_f + fl(yl_b) + d * xl
    y_c = (yc_f + fl(yc_b) + d * xc) if with_ctx else None
    return y_l, y_c


def ssm_out(y, z, p):
    B, L = y.shape[:2]
    y = y.reshape(B, L, SSM_INNER) * jax.nn.silu(z)
    return rmsnorm(y, p["ssm_norm"]) @ p["ssm_w_o"]


def spatial_gating(uv, p):
    B, L, _ = uv.shape
    u, v = jnp.split(jax.nn.gelu(uv), 2, axis=-1)
    v = layernorm(v, p["gm_norm"])
    nc = L // GM_CHUNK
    vg = v.reshape(B, nc, GM_CHUNK, GM_GROUPS, GM_GDIM)
    mixed = jnp.einsum('gij,bcjgd->bcigd', p["gm_w_s"], vg) + p["gm_b_s"].T[:, :, None]
    return (u * mixed.reshape(B, L, GM_WIDTH)) @ p["gm_w_o"]


def merge(o_mla, o_ssm, o_gm, gate_raw, p):
    g_a, g_b, g_c = jnp.split(jax.nn.sigmoid(gate_raw + p["b_gate"]), N_BRANCH, axis=-1)
    return (g_a * o_mla + g_b * o_ssm + g_c * o_gm) @ p["w_out"]


def token_mixer(h_lat, h_ctx, cos, sin, p, with_ctx):
    B, L, _ = h_lat.shape
    w_in = p["w_in"]
    proj_lat = h_lat @ w_in
    proj_ctx = h_ctx @ (w_in if with_ctx else w_in[:, :KV_SIDE])
    kv_l, kr_l, xbc_l, dtf_l, dtb_l = _split(proj_lat[..., :KV_SIDE], KV_SIZES)
    kv_c, kr_c, xbc_c, dtf_c, dtb_c = _split(proj_ctx[..., :KV_SIDE], KV_SIZES)
    q_l, z_l, uv_l, gr_l = _split(proj_lat[..., KV_SIDE:], Q_SIZES)

    kn_l, v_l = mla_keys(kv_l, p)
    kn_c, v_c = mla_keys(kv_c, p)
    kr_l = apply_rope(kr_l, cos, sin)
    qn_l, qr_l = mla_queries(q_l, p)
    qr_l = apply_rope(qr_l, cos[:, None, :], sin[:, None, :])
    a_l = block_attention(qn_l, qr_l, jnp.concatenate([kn_l, kn_c], axis=1),
                          jnp.concatenate([kr_l, kr_c], axis=1), jnp.concatenate([v_l, v_c], axis=1))
    o_mla_l = a_l @ p["mla_w_o"]

    y_ssm_l, y_ssm_c = ssm_bidir(ssm_prep(xbc_l, dtf_l, dtb_l, p), ssm_prep(xbc_c, dtf_c, dtb_c, p), p, with_ctx)
    o_ssm_l = ssm_out(y_ssm_l, z_l, p)

    o_gm_l = spatial_gating(uv_l, p)
    y_lat = merge(o_mla_l, o_ssm_l, o_gm_l, gr_l, p)

    if not with_ctx:
        return y_lat, None
    q_c, z_c, uv_c, gr_c = _split(proj_ctx[..., KV_SIDE:], Q_SIZES)
    qn_c, qr_c = mla_queries(q_c, p)
    o_mla_c = block_attention(qn_c, qr_c, kn_c, kr_c, v_c) @ p["mla_w_o"]
    o_ssm_c = ssm_out(y_ssm_c, z_c, p)
    o_gm_c = spatial_gating(uv_c, p)
    y_ctx = merge(o_mla_c, o_ssm_c, o_gm_c, gr_c, p)
    return y_lat, y_ctx


def setup_inputs(seed: int = 0) -> dict:
    key = jax.random.key(seed)
    ks = iter(jax.random.split(key, 48))
    f32 = jnp.float32
    L = DEPTH
    D = D_MODEL

    def dense(shape, fan_in, gain=1.0):
        return jax.random.normal(next(ks), shape, f32) * (gain * fan_in ** -0.5)

    def gains(shape):
        return 1.0 + 0.05 * jax.random.normal(next(ks), shape, f32)

    def small(shape):
        return 0.01 * jax.random.normal(next(ks), shape, f32)

    dt0 = jnp.exp(jax.random.uniform(next(ks), (L, 2, SSM_HEADS), f32, math.log(1e-3), math.log(1e-1)))
    a0 = jax.random.uniform(next(ks), (L, 2, SSM_HEADS), f32, 1.0, 16.0)
    return {
        "x": jax.random.normal(next(ks), (BATCH, SEQ, D), f32),
        "c": jax.random.normal(next(ks), (BATCH, D), f32),
        "ctx": jax.random.normal(next(ks), (BATCH, CTX_LEN, D), f32),
        "c_ctx": jax.random.normal(next(ks), (D,), f32),
        "w_ada": dense((L, D, N_MOD * D), D, 0.5),
        "b_ada": small((L, N_MOD * D)),
        "norm_g": gains((L, 3, D)),
        "ffn1_w_in": dense((L, D, 2 * D_FF), D),
        "ffn1_w_out": dense((L, D_FF, D), D_FF),
        "ffn2_w_in": dense((L, D, 2 * D_FF), D),
        "ffn2_w_out": dense((L, D_FF, D), D_FF),
        "w_in": dense((L, D, D_IN), D),
        "mla_q_norm": gains((L, MLA_Q_RANK)),
        "mla_w_uq": dense((L, MLA_Q_RANK, MLA_HEADS * (MLA_NOPE + MLA_ROPE)), MLA_Q_RANK),
        "mla_kv_norm": gains((L, MLA_KV_RANK)),
        "mla_w_ukv": dense((L, MLA_KV_RANK, MLA_HEADS * (MLA_NOPE + MLA_V)), MLA_KV_RANK),
        "mla_w_o": dense((L, MLA_HEADS * MLA_V, D), MLA_HEADS * MLA_V),
        "ssm_conv_w": dense((L, XBC_DIM, SSM_CONV), SSM_CONV),
        "ssm_conv_b": small((L, XBC_DIM)),
        "ssm_a_log": jnp.log(a0),
        "ssm_dt_bias": dt0 + jnp.log(-jnp.expm1(-dt0)),
        "ssm_d": gains((L, SSM_HEADS)),
        "ssm_norm": gains((L, SSM_INNER)),
        "ssm_w_o": dense((L, SSM_INNER, D), SSM_INNER),
        "gm_norm": gains((L, GM_WIDTH)),
        "gm_w_s": dense((L, GM_GROUPS, GM_CHUNK, GM_CHUNK), GM_CHUNK),
        "gm_b_s": gains((L, GM_GROUPS, GM_CHUNK)),
        "gm_w_o": dense((L, GM_WIDTH, D), GM_WIDTH),
        "b_gate": small((L, N_BRANCH * D)),
        "w_out": dense((L, D, D), D),
        "final_norm": gains((D,)),
    }


def reference(x, c, ctx, c_ctx, w_ada, b_ada, norm_g, ffn1_w_in, ffn1_w_out, ffn2_w_in, ffn2_w_out,
              w_in, mla_q_norm, mla_w_uq, mla_kv_norm, mla_w_ukv, mla_w_o,
              ssm_conv_w, ssm_conv_b, ssm_a_log, ssm_dt_bias, ssm_d, ssm_norm, ssm_w_o,
              gm_norm, gm_w_s, gm_b_s, gm_w_o, b_gate, w_out, final_norm):
    rows = x.shape[1] // GRID_W
    cos, sin = axial_rope_tables(rows)
    for l in range(DEPTH):
        last = l == DEPTH - 1
        p = {
            "w_in": w_in[l], "mla_q_norm": mla_q_norm[l], "mla_w_uq": mla_w_uq[l],
            "mla_kv_norm": mla_kv_norm[l], "mla_w_ukv": mla_w_ukv[l], "mla_w_o": mla_w_o[l],
            "ssm_conv_w": ssm_conv_w[l], "ssm_conv_b": ssm_conv_b[l], "ssm_a_log": ssm_a_log[l],
            "ssm_dt_bias": ssm_dt_bias[l], "ssm_d": ssm_d[l], "ssm_norm": ssm_norm[l], "ssm_w_o": ssm_w_o[l],
            "gm_norm": gm_norm[l], "gm_w_s": gm_w_s[l], "gm_b_s": gm_b_s[l], "gm_w_o": gm_w_o[l],
            "b_gate": b_gate[l], "w_out": w_out[l],
        }
        mx = jnp.split((jax.nn.silu(c) @ w_ada[l] + b_ada[l])[:, None, :], N_MOD, axis=-1)
        mc = jnp.split(jax.nn.silu(c_ctx) @ w_ada[l] + b_ada[l], N_MOD, axis=-1)

        x = x + 0.5 * mx[2] * swiglu(pre(x, norm_g[l, 0], mx[0], mx[1]), ffn1_w_in[l], ffn1_w_out[l])
        ctx = ctx + 0.5 * mc[2] * swiglu(pre(ctx, norm_g[l, 0], mc[0], mc[1]), ffn1_w_in[l], ffn1_w_out[l])

        y_x, y_c = token_mixer(pre(x, norm_g[l, 1], mx[3], mx[4]), pre(ctx, norm_g[l, 1], mc[3], mc[4]),
                               cos, sin, p, not last)
        x = x + mx[5] * y_x

        x = x + 0.5 * mx[8] * swiglu(pre(x, norm_g[l, 2], mx[6], mx[7]), ffn2_w_in[l], ffn2_w_out[l])
        if not last:
            ctx = ctx + mc[5] * y_c
            ctx = ctx + 0.5 * mc[8] * swiglu(pre(ctx, norm_g[l, 2], mc[6], mc[7]), ffn2_w_in[l], ffn2_w_out[l])
    return rmsnorm(x, final_norm)
```

```python
import numpy as np
from contextlib import ExitStack
import concourse.bass as bass
import concourse.mybir as mybir
from concourse.bass_utils import run_bass_kernel_spmd

F32 = mybir.dt.float32
BF16 = mybir.dt.bfloat16
AF = mybir.ActivationFunctionType
ALU = mybir.AluOpType

ENGS = ("pe", "act", "dve", "pool", "sp")
EPOCH = 30000

D = 1024
TL = 2048
TCX = 256
T = TL + TCX
DEPTH = 2
DFF = 2816
DIN = 6064
KV_SIDE = 1200
EPS = 1e-6
ATTN_SCALE = 96 ** -0.5
TCH = [(0, 512), (512, 512), (1024, 512), (1536, 512), (2048, 256)]
OFF_Q = KV_SIDE
OFF_Z = KV_SIDE + 256
OFF_UV = OFF_Z + 512
OFF_G = OFF_UV + 1024


class Prog:
    def __init__(self, nc, stack):
        self.nc = nc
        self.stack = stack
        self.ops = {e: [] for e in ENGS}
        self.seen = {e: {} for e in ENGS}
        self.buf = {}
        self.esem = {e: [] for e in ENGS}
        self.dma_sem = {}

    def _need(self, eng, token, waits):
        if token is None:
            return
        if token[0] == "eng":
            _, f, idx = token
            if f == "pe" and eng == "pe":
                return
            k = ("eng", f)
            if self.seen[eng].get(k, -1) >= idx:
                return
            self.seen[eng][k] = idx
            waits.append(token)
        else:
            _, key, val = token
            k = ("dma", key)
            if self.seen[eng].get(k, -1) >= val:
                return
            self.seen[eng][k] = val
            waits.append(token)

    def _deps(self, eng, reads, writes):
        waits = []
        for k in reads:
            st = self.buf.get(k)
            if st is not None:
                self._need(eng, st[0], waits)
        for k in writes:
            st = self.buf.get(k)
            if st is not None:
                self._need(eng, st[0], waits)
                for t in st[1].values():
                    self._need(eng, t, waits)
        return waits

    def _commit(self, token, reads, writes):
        src = (token[0], token[1])
        for k in reads:
            st = self.buf.setdefault(k, [None, {}])
            st[1][src] = token
        for k in writes:
            self.buf[k] = [token, {}]

    def op(self, eng, fn, reads=(), writes=(), signal=True):
        waits = self._deps(eng, reads, writes)
        idx = len(self.ops[eng])
        self.ops[eng].append(["op", waits, fn, signal])
        self._commit(("eng", eng, idx), reads, writes)
        return idx

    def dma(self, eng, out_ap, in_ap, reads=(), writes=(), key=None, **kw):
        if key is None:
            key = writes[0] if writes else ("st", reads[0])
        waits = self._deps(eng, reads, writes)
        ent = self.dma_sem.get(key)
        if ent is None:
            ent = [self.stack.enter_context(self.nc.semaphore(f"s_dma{len(self.dma_sem)}")), 0]
            self.dma_sem[key] = ent
        ent[1] += 16
        val = ent[1]
        self.ops[eng].append(["dma", waits, (out_ap, in_ap, kw), (ent[0], val)])
        self._commit(("dma", key, val), reads, writes)

    def wait_all_dma(self, eng, keys):
        waits = []
        for key in keys:
            ent = self.dma_sem[key]
            self._need(eng, ("dma", key, ent[1]), waits)
        self.ops[eng].append(["wait", waits, None, False])

    def barrier(self):
        for e in ENGS:
            waits = []
            for f in ENGS:
                if f == e:
                    continue
                last = None
                for i in range(len(self.ops[f]) - 1, -1, -1):
                    if self.ops[f][i][0] == "op":
                        last = i
                        break
                if last is not None:
                    self.ops[f][last][3] = True
                    self._need(e, ("eng", f, last), waits)
            for key, ent in self.dma_sem.items():
                self._need(e, ("dma", key, ent[1]), waits)
            self.ops[e].append(["wait", waits, None, False])

    def emit(self):
        nc = self.nc
        signum = {}
        for e in ENGS:
            ops = self.ops[e]
            for i in range(len(ops) - 1, -1, -1):
                if ops[i][0] == "op":
                    ops[i][3] = True
                    break
            nums = [0] * len(ops)
            c = 0
            for i, o in enumerate(ops):
                if o[0] == "op" and o[3]:
                    c += 1
                nums[i] = c
            res = [None] * len(ops)
            nxt = None
            for i in range(len(ops) - 1, -1, -1):
                if ops[i][0] == "op" and ops[i][3]:
                    nxt = nums[i]
                res[i] = nxt
            signum[e] = (nums, res)
            nep = (c + EPOCH - 1) // EPOCH
            self.esem[e] = [self.stack.enter_context(nc.semaphore(f"s_{e}{i}")) for i in range(max(nep, 1))]

        def sem_for(e, n):
            return self.esem[e][(n - 1) // EPOCH], ((n - 1) % EPOCH) + 1

        def run_engine(e, eh):
            ops = self.ops[e]
            nums, _ = signum[e]
            for i, o in enumerate(ops):
                kind, waits = o[0], o[1]
                for t in waits:
                    if t[0] == "eng":
                        _, f, idx = t
                        n = signum[f][1][idx]
                        assert n is not None
                        s, v = sem_for(f, n)
                        eh.wait_ge(s, v)
                    else:
                        _, key, val = t
                        eh.wait_ge(self.dma_sem[key][0], val)
                if kind == "op":
                    ins = o[2](eh)
                    if o[3]:
                        s, v = sem_for(e, nums[i])
                        ins.then_inc(s, 1)
                elif kind == "dma":
                    out_ap, in_ap, kw = o[2]
                    s, v = o[3]
                    eh.dma_start(out=out_ap, in_=in_ap, **kw).then_inc(s, 16)

        with nc.Block() as block:
            @block.tensor
            def _(eh):
                run_engine("pe", eh)

            @block.scalar
            def _(eh):
                run_engine("act", eh)

            @block.vector
            def _(eh):
                run_engine("dve", eh)

            @block.gpsimd
            def _(eh):
                run_engine("pool", eh)

            @block.sync
            def _(eh):
                run_engine("sp", eh)


IN_SPECS = [
    ("x", [TL, D]), ("crow", [16, 128]), ("ctx", [TCX, D]),
    ("w_ada", [DEPTH, D, 9 * D]), ("b_ada", [DEPTH, 72, 128]), ("norm_g", [DEPTH, 24, 128]),
    ("ffn1_w_in", [DEPTH, D, 2 * DFF]), ("ffn1_w_out", [DEPTH, DFF, D]),
    ("ffn2_w_in", [DEPTH, D, 2 * DFF]), ("ffn2_w_out", [DEPTH, DFF, D]),
    ("w_in", [DEPTH, D, DIN]), ("mla_q_norm", [DEPTH, 2, 128]), ("mla_w_uq", [DEPTH, 256, 768]),
    ("mla_kv_norm", [DEPTH, 1, 128]), ("mla_w_ukv", [DEPTH, 128, 1024]), ("mla_w_o", [DEPTH, 512, D]),
    ("ssm_conv_w", [DEPTH, 1024, 5]), ("ssm_conv_b", [DEPTH, 8, 128]), ("ssm_small", [DEPTH, 1, 40]),
    ("ssm_norm", [DEPTH, 4, 128]), ("ssm_w_o", [DEPTH, 512, D]),
    ("gm_norm", [DEPTH, 4, 128]), ("gm_w_s", [DEPTH, 8, 128, 128]), ("gm_b_s", [DEPTH, 8, 128]),
    ("gm_w_o", [DEPTH, 512, D]), ("b_gate", [DEPTH, 24, 128]), ("w_out", [DEPTH, D, D]),
    ("final_norm", [8, 128]),
    ("c_ident", [128, 128]), ("c_masks", [4, 128, 128]), ("c_rope", [2, 32, TL]), ("c_sel", [2, 128]),
]


class Builder:
    def __init__(self, nc, P, dbg=None, stop_after=None):
        self.nc = nc
        self.P = P
        self.off = 16512
        self.top = 229344
        self.uid = 0
        self.dbg = dbg
        self.stop_after = stop_after
        self.io = {}
        for name, shape in IN_SPECS:
            self.io[name] = nc.dram_tensor(name, shape, F32, kind="ExternalInput").ap()
        self.out = nc.dram_tensor("out", [TL, D], F32, kind="ExternalOutput").ap()
        self.yf_scr = nc.dram_tensor("yf_scr", [18, 128, 512], BF16, kind="Internal").ap()
        self.xbc_scr = nc.dram_tensor("xbc_scr", [8, 128, T], BF16, kind="Internal").ap()
        self.m_scr = nc.dram_tensor("m_scr", [8, 128, T], BF16, kind="Internal").ap()
        self.ps = [nc.alloc_psum_tensor(f"psb{i}", [128, 512], F32) for i in range(8)]
        self.rr = 0

    def sb(self, name, shape, dtype):
        esz = 4 if dtype == F32 else 2
        n = 1
        for s in shape[1:]:
            n *= s
        nbytes = (n * esz + 31) // 32 * 32
        assert self.off + nbytes <= self.top, (name, self.off, nbytes)
        self.uid += 1
        t = self.nc.alloc_sbuf_tensor_at(f"{name}_{self.uid}", shape, dtype, offset=self.off)
        self.off += nbytes
        return t

    def psv(self, i, dtype=F32):
        ap = self.ps[i][:]
        if dtype != F32:
            ap = ap.bitcast(dtype)
        return ap

    def mm(self, out, lhsT, rhs, start, stop, reads, writes, signal=None):
        if signal is None:
            signal = stop
        self.P.op("pe", lambda e: e.matmul(out, lhsT, rhs, start=start, stop=stop),
                  reads=reads, writes=writes, signal=signal)

    def tr(self, out, in_, ident, reads, writes, signal=True):
        self.P.op("pe", lambda e: e.transpose(out, in_, ident), reads=reads, writes=writes, signal=signal)

    def act(self, out, in_, func, reads, writes, bias=None, scale=None):
        kw = {}
        if bias is not None:
            kw["bias"] = bias
        if scale is not None:
            kw["scale"] = scale
        self.P.op("act", lambda e: e.activation(out, in_, func, **kw), reads=reads, writes=writes)

    def tt(self, out, a, b, op, reads, writes, eng="dve"):
        self.P.op(eng, lambda e: e.tensor_tensor(out, a, b, op), reads=reads, writes=writes)

    def ts(self, out, a, s1, s2, op0, op1, reads, writes, eng="dve"):
        if s2 is None:
            self.P.op(eng, lambda e: e.tensor_scalar(out, a, s1, None, op0), reads=reads, writes=writes)
        else:
            self.P.op(eng, lambda e: e.tensor_scalar(out, a, s1, s2, op0, op1), reads=reads, writes=writes)

    def stt(self, out, a, s, b, op0, op1, reads, writes, eng="dve"):
        self.P.op(eng, lambda e: e.scalar_tensor_tensor(out, a, s, b, op0, op1), reads=reads, writes=writes)

    def cp(self, out, in_, reads, writes, eng=None):
        if eng is None:
            self.rr ^= 1
            eng = "dve" if self.rr else "act"
        if eng == "act":
            self.P.op("act", lambda e: e.copy(out, in_), reads=reads, writes=writes)
        else:
            self.P.op(eng, lambda e: e.tensor_copy(out, in_), reads=reads, writes=writes)

    def memset(self, ap, val, writes, eng="dve"):
        self.P.op(eng, lambda e: e.memset(ap, val), writes=writes)

    def dump(self, name, ap, reads):
        if self.dbg is None or name not in self.dbg:
            return
        d = self.dbg[name]
        self.P.dma("sp", d, ap, reads=reads, key="dbg")

    def build(self):
        nc, P, io = self.nc, self.P, self.io
        self.xres = self.sb("xres", [128, 8, T], F32)
        self.hn = self.sb("hn", [128, 8, T], BF16)
        self.ident = self.sb("ident", [128, 128], F32)
        self.identb = self.sb("identb", [128, 128], BF16)
        self.ones_f = self.sb("ones_f", [128, 128], F32)
        self.ones_b = self.sb("ones_b", [128, 128], BF16)
        self.masks = self.sb("masks", [128, 4, 128], F32)
        self.masks_b = self.sb("masks_b", [128, 2, 128], BF16)
        self.eps_t = self.sb("eps_t", [128, 1], F32)
        self.silc = self.sb("silc", [128, 16], F32)
        self.silcb = self.sb("silcb", [128, 16], BF16)
        self.LP = []
        for l in range(DEPTH):
            self.LP.append(dict(
                colsA=self.sb("colsA", [128, 128], F32),
                colsB=self.sb("colsB", [128, 32], F32),
                mod=self.sb("mod", [128, 72, 2], F32),
                nA=self.sb("nA", [128, 3, 8, 2], F32),
                gsc=self.sb("gsc", [128, 3, 8, 2], F32),
                ssm_bc=self.sb("ssm_bc", [128, 40], F32),
                expA=self.sb("expA", [128, 16], F32),
                convw=self.sb("convw", [128, 8, 5], F32)))
        self.persist_off = self.off

        P.dma("sp", self.ident[:], io["c_ident"], writes=["ident"])
        P.dma("sp", self.masks[:], io["c_masks"].rearrange("m s t -> s m t"), writes=["masks"])
        self.cp(self.identb[:], self.ident[:], ["ident"], ["identb"], eng="dve")
        self.memset(self.ones_f[:], 1.0, ["ones_f"])
        self.memset(self.ones_b[:], 1.0, ["ones_b"])
        self.memset(self.eps_t[:], EPS, ["eps_t"])
        self.cp(self.masks_b[:, 0, :], self.masks[:, 0, :], ["masks"], ["masks_b"], eng="dve")
        self.cp(self.masks_b[:, 1, :], self.masks[:, 2, :], ["masks"], ["masks_b"], eng="dve")

        mp = self.off
        self.wada = [self.sb(f"wada{i}", [128, 8, 1024], BF16) for i in range(3)]
        steps = self.params_steps(0)
        steps[0]()
        self.load_x(barrier=False)
        for f in steps[1:]:
            f()
        self.P.barrier()
        self.off = mp
        if self.stop_after == "load":
            return self.finish_dbg()
        for l in range(DEPTH):
            self.layer(l)
            if self.stop_after is not None and self.stop_after.startswith(f"L{l}"):
                return self.finish_dbg()
        self.final()
        return None

    def finish_dbg(self):
        if "dbg" in self.P.dma_sem:
            self.P.wait_all_dma("sp", ["dbg"])

    def load_x(self, barrier=True):
        P, io = self.P, self.io
        m = self.off
        stg = [self.sb(f"xstg{i}", [128, D], F32) for i in range(2)]
        for tt in range(18):
            s = stg[tt % 2]
            sk = f"xstg{tt % 2}"
            src = io["x"][tt * 128:(tt + 1) * 128, :] if tt < 16 else io["ctx"][(tt - 16) * 128:(tt - 15) * 128, :]
            P.dma("sp", s[:], src, writes=[sk])
            for half in range(2):
                pb = 4 + (tt * 2 + half) % 4
                for k in range(4):
                    dc = half * 4 + k
                    self.tr(self.psv(pb)[:, k * 128:(k + 1) * 128], s[:, dc * 128:(dc + 1) * 128], self.ident[:],
                            reads=[sk, "ident"], writes=[("ps", pb)], signal=(k == 3))
                self.cp(self.xres[:, half * 4:half * 4 + 4, tt * 128:(tt + 1) * 128],
                        self.psv(pb).rearrange("p (k t) -> p k t", k=4), [("ps", pb)], [("xres", tt // 4)], eng="dve")
        if barrier:
            self.P.barrier()
            self.off = m

    def set_layer(self, l):
        for k, v in self.LP[l].items():
            setattr(self, k, v)

    def params_steps(self, l):
        P, io = self.P, self.io
        lp = self.LP[l]
        colsA, colsB, mod, nA, gsc, ssm_bc, expA, convw = (lp[k] for k in
                                                          ("colsA", "colsB", "mod", "nA", "gsc", "ssm_bc", "expA", "convw"))
        K = lambda n: f"{n}{l}"
        st = {}

        def slab_dma(j):
            jj = l * 9 + j
            P.dma("pool", self.wada[jj % 3][:],
                  io["w_ada"][l, :, j * 1024:(j + 1) * 1024].rearrange("(kc p) c -> p kc c", p=128),
                  writes=[f"wada{jj % 3}"])

        def s0():
            rows = self.sb("rows", [128, 128], F32)
            rows2 = self.sb("rows2", [32, 128], F32)
            small = self.sb("small", [1, 40], F32)
            q = "act"
            P.dma(q, rows[0:72, :], io["b_ada"][l], writes=[K("rows")], key=K("rows"))
            P.dma(q, rows[72:96, :], io["norm_g"][l], writes=[K("rows")], key=K("rows"))
            P.dma(q, rows[96:120, :], io["b_gate"][l], writes=[K("rows")], key=K("rows"))
            P.dma(q, rows[120:128, :], io["ssm_conv_b"][l], writes=[K("rows")], key=K("rows"))
            P.dma(q, rows2[0:4, :], io["ssm_norm"][l], writes=[K("rows2")], key=K("rows2"))
            P.dma(q, rows2[4:8, :], io["gm_norm"][l], writes=[K("rows2")], key=K("rows2"))
            P.dma(q, rows2[8:10, :], io["mla_q_norm"][l], writes=[K("rows2")], key=K("rows2"))
            P.dma(q, rows2[10:11, :], io["mla_kv_norm"][l], writes=[K("rows2")], key=K("rows2"))
            P.dma(q, rows2[11:19, :], io["final_norm"], writes=[K("rows2")], key=K("rows2"))
            P.dma(q, rows2[19:27, :], io["crow"][0:8, :], writes=[K("rows2")], key=K("rows2"))
            P.dma(q, rows2[27:32, :], io["crow"][8:13, :], writes=[K("rows2")], key=K("rows2"))
            P.dma(q, small[:], io["ssm_small"][l], writes=[K("small")])
            P.dma(q, convw[:], io["ssm_conv_w"][l].rearrange("(cc p) k -> p cc k", p=128),
                  writes=[K("convw")], allow_slow_non_contiguous=True)
            slab_dma(0)
            slab_dma(1)
            if l == 0:
                crow = self.sb("crowt", [16, 128], F32)
                P.dma(q, crow[:], io["crow"], writes=["crowt"])
            self.tr(self.psv(0)[:, 0:128], rows[:], self.ident[:], [K("rows"), "ident"], [("ps", 0)])
            self.cp(colsA[:], self.psv(0)[:, 0:128], [("ps", 0)], [K("colsA")], eng="dve")
            self.tr(self.psv(1)[:, 0:32], rows2[:], self.ident[0:32, 0:32], [K("rows2"), "ident"], [("ps", 1)])
            self.cp(colsB[:], self.psv(1)[:, 0:32], [("ps", 1)], [K("colsB")], eng="dve")
            self.mm(self.psv(2)[:, 0:40], self.ones_f[0:1, :], small[:], True, True, ["ones_f", K("small")], [("ps", 2)])
            self.cp(ssm_bc[:], self.psv(2)[:, 0:40], [("ps", 2)], [K("ssm_bc")], eng="dve")
            self.act(expA[:], ssm_bc[:, 0:16], AF.Exp, [K("ssm_bc")], [K("expA")])
            if l == 0:
                self.tr(self.psv(3)[:, 0:16], crow[:], self.ident[0:16, 0:16], ["crowt", "ident"], [("ps", 3)])
                self.act(self.silc[:], self.psv(3)[:, 0:16], AF.Silu, [("ps", 3)], ["silc"])
                self.cp(self.silcb[:], self.silc[:], ["silc"], ["silcb"], eng="dve")

        def sj(j):
            if j + 2 < 9:
                slab_dma(j + 2)
            jj = l * 9 + j
            wb = self.wada[jj % 3]
            wk = f"wada{jj % 3}"
            silc3 = self.silcb[:].rearrange("p (w k) -> p w k", w=2)
            pb = 4 + (j % 2)
            for dc in range(8):
                for kc in range(8):
                    self.mm(self.psv(pb)[:, dc * 2:dc * 2 + 2], wb[:, kc, dc * 128:(dc + 1) * 128], silc3[:, :, kc],
                            kc == 0, kc == 7, [wk, "silcb"], [("ps", pb)], signal=(kc == 7 and dc == 7))
            self.tt(mod[:, j * 8:(j + 1) * 8, :], self.psv(pb)[:, 0:16].rearrange("p (a b) -> p a b", b=2),
                    colsA[:, j * 8:(j + 1) * 8].unsqueeze(2).to_broadcast([128, 8, 2]), ALU.add,
                    [("ps", pb), K("colsA")], [K("mod")])

        def sl():
            for j in range(3):
                g_bc = colsA[:, 72 + j * 8:72 + (j + 1) * 8].unsqueeze(2).to_broadcast([128, 8, 2])
                self.stt(nA[:, j], mod[:, (3 * j + 1) * 8:(3 * j + 2) * 8, :], 1.0, g_bc, ALU.add, ALU.mult,
                         [K("mod"), K("colsA")], [K("nA")])
                self.ts(gsc[:, j], mod[:, (3 * j + 2) * 8:(3 * j + 3) * 8, :], 1.0 if j == 1 else 0.5, None,
                        ALU.mult, None, [K("mod")], [K("gsc")])

        return [s0] + [(lambda j=j: sj(j)) for j in range(9)] + [sl]

    def prenorm(self, j, nti=5):
        P = self.P
        m = self.off
        sq = [self.sb(f"sq{i}", [128, 8, 512], BF16) for i in range(2)]
        rs = [self.sb(f"rs{i}", [128, 512], F32) for i in range(2)]
        tmp = [self.sb(f"ntmp{i}", [128, 512], F32) for i in range(2)]
        for ti, (t0, tn) in enumerate(TCH[:nti]):
            who = 1 if ti == 4 else 0
            b = ti % 2
            xk = ("xres", ti)
            self.act(sq[b][:, :, 0:tn], self.xres[:, :, t0:t0 + tn], AF.Square, [xk], [f"sq{b}"])
            pb = b
            for dc in range(8):
                self.mm(self.psv(pb)[:, 0:tn], self.ones_b[:], sq[b][:, dc, 0:tn], dc == 0, dc == 7,
                        ["ones_b", f"sq{b}"], [("ps", pb)])
            self.act(rs[b][:, 0:tn], self.psv(pb)[:, 0:tn], AF.Sqrt, [("ps", pb), "eps_t"], [f"rs{b}"],
                     bias=self.eps_t[:, 0:1], scale=1.0 / D)
            self.P.op("dve", lambda e, b=b, tn=tn: e.reciprocal(rs[b][:, 0:tn], rs[b][:, 0:tn]),
                      reads=[f"rs{b}"], writes=[f"rs{b}"])
            for dc in range(8):
                tb = dc % 2
                self.stt(tmp[tb][:, 0:tn], self.xres[:, dc, t0:t0 + tn], self.nA[:, j, dc, who:who + 1], rs[b][:, 0:tn],
                         ALU.mult, ALU.mult, [xk, "nA", f"rs{b}"], [f"ntmp{tb}"])
                if dc % 2 == 0:
                    self.act(self.hn[:, dc, t0:t0 + tn], tmp[tb][:, 0:tn], AF.Identity, [f"ntmp{tb}", "mod"], [("hn", ti)],
                             bias=self.mod[:, 3 * j * 8 + dc, who:who + 1])
                else:
                    self.ts(self.hn[:, dc, t0:t0 + tn], tmp[tb][:, 0:tn], self.mod[:, 3 * j * 8 + dc, who:who + 1], None,
                            ALU.add, None, [f"ntmp{tb}", "mod"], [("hn", ti)], eng="pool")
        self.P.barrier()
        self.off = m

    def ffn(self, l, which, extra=None):
        P, io = self.P, self.io
        j = 0 if which == 1 else 2
        w_in = io[f"ffn{which}_w_in"][l]
        w_out = io[f"ffn{which}_w_out"][l]
        self.prenorm(j, nti=(4 if (l == DEPTH - 1 and which == 2) else 5))
        m = self.off
        NG = 11
        wg = [self.sb(f"fwg{i}", [128, 8, 256], BF16) for i in range(2)]
        wu = [self.sb(f"fwu{i}", [128, 8, 256], BF16) for i in range(2)]
        wo = [self.sb(f"fwo{i}", [128, 2, 1024], BF16) for i in range(2)]
        sg = [self.sb(f"fsg{i}", [128, 512], F32) for i in range(2)]
        at = [self.sb(f"fat{i}", [128, 2, 512], BF16) for i in range(2)]
        extra = list(extra) if extra else []
        if extra:
            self.wada = [self.sb(f"wada{i}", [128, 8, 1024], BF16) for i in range(3)]

        def load(g):
            b = g % 2
            P.dma("pool", wg[b][:], w_in[:, g * 256:(g + 1) * 256].rearrange("(kc p) c -> p kc c", p=128),
                  writes=[f"fwg{b}"])
            P.dma("pool", wu[b][:], w_in[:, DFF + g * 256:DFF + (g + 1) * 256].rearrange("(kc p) c -> p kc c", p=128),
                  writes=[f"fwu{b}"])
            P.dma("pool", wo[b][:], w_out[g * 256:(g + 1) * 256, :].rearrange("(hc p) d -> p hc d", p=128),
                  writes=[f"fwo{b}"])

        load(0)
        nti = 4 if (l == DEPTH - 1 and which == 2) else 5
        iters = [(g, ti) for g in range(NG) for ti in range(nti)]

        def up(i):
            g, ti = iters[i]
            t0, tn = TCH[ti]
            b, ab = g % 2, i % 2
            for hc in range(2):
                pg, pu = 0 + hc, 2 + hc
                for kc in range(8):
                    self.mm(self.psv(pg)[:, 0:tn], wg[b][:, kc, hc * 128:(hc + 1) * 128], self.hn[:, kc, t0:t0 + tn],
                            kc == 0, kc == 7, [f"fwg{b}", ("hn", ti)], [("ps", pg)])
                for kc in range(8):
                    self.mm(self.psv(pu)[:, 0:tn], wu[b][:, kc, hc * 128:(hc + 1) * 128], self.hn[:, kc, t0:t0 + tn],
                            kc == 0, kc == 7, [f"fwu{b}", ("hn", ti)], [("ps", pu)])
                self.act(sg[hc][:, 0:tn], self.psv(pg)[:, 0:tn], AF.Silu, [("ps", pg)], [f"fsg{hc}"])
                self.tt(at[ab][:, hc, 0:tn], sg[hc][:, 0:tn], self.psv(pu)[:, 0:tn], ALU.mult,
                        [f"fsg{hc}", ("ps", pu)], [f"fat{ab}"])

        def down(i):
            g, ti = iters[i]
            t0, tn = TCH[ti]
            who = 1 if ti == 4 else 0
            b, ab = g % 2, i % 2
            for dc in range(8):
                po = 4 + (dc % 4)
                for hc in range(2):
                    self.mm(self.psv(po)[:, 0:tn], wo[b][:, hc, dc * 128:(dc + 1) * 128], at[ab][:, hc, 0:tn],
                            hc == 0, hc == 1, [f"fwo{b}", f"fat{ab}"], [("ps", po)])
                self.stt(self.xres[:, dc, t0:t0 + tn], self.psv(po)[:, 0:tn], self.gsc[:, j, dc, who:who + 1],
                         self.xres[:, dc, t0:t0 + tn], ALU.mult, ALU.add, [("ps", po), "gsc", ("xres", ti)],
                         [("xres", ti)])

        for i in range(len(iters) + 1):
            if i < len(iters):
                up(i)
            if i >= 1:
                down(i - 1)
            if i < len(iters) and iters[i][1] == 0 and iters[i][0] + 1 < NG:
                load(iters[i][0] + 1)
            if extra and i % 4 == 3:
                extra.pop(0)()
        while extra:
            extra.pop(0)()
        self.P.barrier()
        self.off = m

    def tail(self, l, br, brT, w_o_name, mode):
        P, io = self.P, self.io
        m = self.off
        wgt = self.sb("twg", [128, 8, 1024], BF16)
        wo = self.sb("two", [128, 4, 1024], BF16)
        sig = [self.sb(f"tsig{i}", [128, 512], F32) for i in range(2)]
        mt = [self.sb(f"tm{i}", [128, 8, 512], BF16) for i in range(2)]
        c0 = OFF_G + br * 1024
        for dc in range(8):
            P.dma("pool", wo[:, :, dc * 128:(dc + 1) * 128],
                  io[w_o_name][l][:, dc * 128:(dc + 1) * 128].rearrange("(kc p) c -> p kc c", p=128), writes=[("two", dc)])
            P.dma("pool", wgt[:, :, dc * 128:(dc + 1) * 128],
                  io["w_in"][l][:, c0 + dc * 128:c0 + (dc + 1) * 128].rearrange("(kc p) c -> p kc c", p=128),
                  writes=[("twg", dc)])
        if mode == "last":
            wout = self.sb("twout", [128, 8, 1024], BF16)
            P.dma("pool", wout[:], io["w_out"][l].rearrange("(kc p) c -> p kc c", p=128), writes=["twout"])
            mprev = [self.sb("tmp0", [128, 8, 512], BF16)] * 2
            mpk = ["tmp0", "tmp0"]
        elif mode == "mid":
            mprev = [self.sb(f"tmp{i}", [128, 8, 512], BF16) for i in range(2)]
            mpk = ["tmp0", "tmp1"]
        msc = self.m_scr.rearrange("dc p t -> p dc t")
        for ti, (t0, tn) in enumerate(TCH):
            if ti == 4 and l == DEPTH - 1:
                continue
            who = 1 if ti == 4 else 0
            mb = ti % 2
            if mode != "first":
                P.dma("act", mprev[mb][:, :, 0:tn], msc[:, :, t0:t0 + tn], reads=[("msc", ti)], writes=[mpk[mb]],
                      allow_slow_non_contiguous=True)
            for dc in range(8):
                po, pg = (dc % 2), 2 + (dc % 2)
                for kc in range(4):
                    self.mm(self.psv(po)[:, 0:tn], wo[:, kc, dc * 128:(dc + 1) * 128], brT[:, kc, t0:t0 + tn],
                            kc == 0, kc == 3, [("two", dc), "brT"], [("ps", po)])
                for kc in range(8):
                    self.mm(self.psv(pg)[:, 0:tn], wgt[:, kc, dc * 128:(dc + 1) * 128], self.hn[:, kc, t0:t0 + tn],
                            kc == 0, kc == 7, [("twg", dc), ("hn", ti)], [("ps", pg)])
                sb_ = dc % 2
                self.act(sig[sb_][:, 0:tn], self.psv(pg)[:, 0:tn], AF.Sigmoid, [("ps", pg), "colsA"], [f"tsig{sb_}"],
                         bias=self.colsA[:, 96 + br * 8 + dc:96 + br * 8 + dc + 1])
                if mode == "first":
                    self.tt(mt[mb][:, dc, 0:tn], sig[sb_][:, 0:tn], self.psv(po)[:, 0:tn], ALU.mult,
                            [f"tsig{sb_}", ("ps", po)], [f"tm{mb}"])
                else:
                    self.tt(sig[sb_][:, 0:tn], sig[sb_][:, 0:tn], self.psv(po)[:, 0:tn], ALU.mult,
                            [f"tsig{sb_}", ("ps", po)], [f"tsig{sb_}"])
                    self.tt(mt[mb][:, dc, 0:tn], sig[sb_][:, 0:tn], mprev[mb][:, dc, 0:tn], ALU.add,
                            [f"tsig{sb_}", mpk[mb]], [f"tm{mb}"], eng="pool")
            if mode != "last":
                P.dma("sp", msc[:, :, t0:t0 + tn], mt[mb][:, :, 0:tn], reads=[f"tm{mb}"], writes=[("msc", ti)],
                      key="msc_st", allow_slow_non_contiguous=True)
                continue
            for d2 in range(8):
                pq = 4 + (d2 % 4)
                for dc in range(8):
                    self.mm(self.psv(pq)[:, 0:tn], wout[:, dc, d2 * 128:(d2 + 1) * 128], mt[mb][:, dc, 0:tn],
                            dc == 0, dc == 7, ["twout", f"tm{mb}"], [("ps", pq)])
                self.stt(self.xres[:, d2, t0:t0 + tn], self.psv(pq)[:, 0:tn], self.gsc[:, 1, d2, who:who + 1],
                         self.xres[:, d2, t0:t0 + tn], ALU.mult, ALU.add, [("ps", pq), "gsc", ("xres", ti)],
                         [("xres", ti)])
        self.P.barrier()
        self.off = m

    def fm_rms(self, src, nk, tn, nfeat, gcols, out_bf, keys_in, key_out, tag, pb):
        sqk = "t_sq"
        self.act(self.t_sq[:, 0:nk, 0:tn], src, AF.Square, keys_in, [sqk])
        for k in range(nk):
            self.mm(self.psv(pb)[:, 0:tn], self.ones_b[:], self.t_sq[:, k, 0:tn], k == 0, k == nk - 1,
                    ["ones_b", sqk], [("ps", pb)])
        self.act(self.t_rs[:, 0:tn], self.psv(pb)[:, 0:tn], AF.Sqrt, [("ps", pb), "eps_t"], ["t_rs"],
                 bias=self.eps_t[:, 0:1], scale=1.0 / nfeat)
        self.P.op("dve", lambda e: e.reciprocal(self.t_rs[:, 0:tn], self.t_rs[:, 0:tn]),
                  reads=["t_rs"], writes=["t_rs"])
        for k in range(nk):
            self.stt(out_bf[:, k, :], src[:, k, :], gcols[:, k:k + 1], self.t_rs[:, 0:tn], ALU.mult, ALU.mult,
                     keys_in + ["colsB", "t_rs"], [key_out])

    def mla(self, l):
        P, io = self.P, self.io
        m = self.off
        w_in = io["w_in"][l]
        brT = self.sb("aT", [128, 4, T], BF16)
        mm_ = self.off
        kvn = self.sb("kvn", [128, T], BF16)
        qn = self.sb("qn", [128, 2, T], BF16)
        kT = [self.sb(f"kT{i}", [96, T], BF16) for i in range(2)]
        rope = self.sb("rope", [96, 2, TL], F32)
        wuq = self.sb("m_wuq", [128, 2, 2, 768], BF16)
        wukv = self.sb("m_wukv", [128, 1024], BF16)
        rtmp = [self.sb(f"m_rt{i}", [96, 512], F32) for i in range(2)]
        m1 = self.off
        wkv = self.sb("m_wkv", [128, 8, 128], BF16)
        wq = self.sb("m_wq", [128, 8, 256], BF16)
        wkr = self.sb("m_wkr", [128, 8, 2, 96], BF16)
        self.t_sq = self.sb("t_sq", [128, 2, 512], BF16)
        self.t_rs = self.sb("t_rs", [128, 512], F32)
        raw = self.sb("m_raw", [128, 2, 512], F32)

        w3 = lambda c0, n: w_in[:, c0:c0 + n].rearrange("(kc p) c -> p kc c", p=128)
        P.dma("pool", wkv[:], w3(0, 128), writes=["m_wkv"])
        P.dma("pool", wq[:], w3(OFF_Q, 256), writes=["m_wq"])
        self.memset(wkr[:], 0.0, ["m_wkr"])
        P.dma("pool", wkr[:, :, 0, 64:96], w3(128, 32), writes=["m_wkr"], key="m_wkr")
        P.dma("pool", wkr[:, :, 1, 64:80], w3(144, 16), writes=["m_wkr"], key="m_wkr")
        P.dma("pool", wkr[:, :, 1, 80:96], w3(128, 16), writes=["m_wkr"], key="m_wkr")
        uq = io["mla_w_uq"][l].rearrange("(kc p) c -> p kc c", p=128)
        P.dma("pool", wuq[:, :, 0, :], uq, writes=["m_wuq"], key="m_wuq")
        P.dma("pool", wuq[:, :, 1, :], uq, writes=["m_wuq"], key="m_wuq")
        uq4 = io["mla_w_uq"][l].rearrange("(kc p) (h c) -> p kc h c", p=128, c=96)
        wuq5 = wuq[:].rearrange("p kc s (h c) -> p kc s h c", c=96)
        for kc in range(2):
            P.dma("pool", wuq5[:, kc, 1, :, 64:80], uq4[:, kc, :, 80:96], reads=["m_wuq"], writes=["m_wuq2"], key="m_wuq2")
            P.dma("pool", wuq5[:, kc, 1, :, 80:96], uq4[:, kc, :, 64:80], reads=["m_wuq"], writes=["m_wuq2"], key="m_wuq2")
        P.dma("pool", wukv[:], io["mla_w_ukv"][l], writes=["m_wukv"])
        P.dma("sp", rope[64:96, 0, :], io["c_rope"][0], writes=["rope"], key="rope")
        P.dma("sp", rope[64:96, 1, :], io["c_rope"][1], writes=["rope"], key="rope")

        rawq2 = self.sb("m_rawq2", [128, 2, 512], F32)
        rawq = [raw, rawq2]
        rawk1 = self.sb("m_rawk1", [128, 1, 512], F32)
        rawk = [rawk1, rawk1]

        def m1_proj(ti):
            t0, tn = TCH[ti]
            hk = ("hn", ti)
            for kc in range(8):
                self.mm(self.psv(0)[:, 0:tn], wkv[:, kc, :], self.hn[:, kc, t0:t0 + tn], kc == 0, kc == 7,
                        ["m_wkv", hk], [("ps", 0)])
            for qc in range(2):
                for kc in range(8):
                    self.mm(self.psv(1 + qc)[:, 0:tn], wq[:, kc, qc * 128:(qc + 1) * 128], self.hn[:, kc, t0:t0 + tn],
                            kc == 0, kc == 7, ["m_wq", hk], [("ps", 1 + qc)])
            for s_ in range(2):
                for kc in range(8):
                    self.mm(self.psv(3 + s_)[0:96, 0:tn], wkr[:, kc, s_, :], self.hn[:, kc, t0:t0 + tn], kc == 0, kc == 7,
                            ["m_wkr", hk], [("ps", 3 + s_)])

        def m1_evac(ti):
            t0, tn = TCH[ti]
            b = ti % 2
            rk, rq = "m_rawk", f"m_rawq{b}"
            self.cp(rawk[b][:, 0, 0:tn], self.psv(0)[:, 0:tn], [("ps", 0)], [rk], eng="act")
            for qc in range(2):
                self.cp(rawq[b][:, qc, 0:tn], self.psv(1 + qc)[:, 0:tn], [("ps", 1 + qc)], [rq], eng="act")
            if ti < 4:
                self.tt(rtmp[0][64:96, 0:tn], self.psv(3)[64:96, 0:tn], rope[64:96, 0, t0:t0 + tn], ALU.mult,
                        [("ps", 3), "rope"], ["m_rt0"])
                self.tt(rtmp[1][64:96, 0:tn], self.psv(4)[64:96, 0:tn], rope[64:96, 1, t0:t0 + tn], ALU.mult,
                        [("ps", 4), "rope"], ["m_rt1"])
                self.tt(kT[0][64:96, t0:t0 + tn], rtmp[0][64:96, 0:tn], rtmp[1][64:96, 0:tn], ALU.add,
                        ["m_rt0", "m_rt1"], ["kT0"])
            else:
                self.cp(kT[0][64:96, t0:t0 + tn], self.psv(3)[64:96, 0:tn], [("ps", 3), ("ps", 4)], ["kT0"], eng="dve")
            self.cp(kT[1][64:96, t0:t0 + tn], kT[0][64:96, t0:t0 + tn], ["kT0"], ["kT1"], eng="pool")

        def m1_chain(ti):
            t0, tn = TCH[ti]
            b = ti % 2
            rk, rq = "m_rawk", f"m_rawq{b}"
            self.fm_rms(rawk[b][:, 0:1, 0:tn], 1, tn, 128, self.colsB[:, 10:11], kvn[:, t0:t0 + tn].unsqueeze(1),
                        [rk], "kvn", "mkv", 7)
            self.fm_rms(rawq[b][:, 0:2, 0:tn], 2, tn, 256, self.colsB[:, 8:10], qn[:, :, t0:t0 + tn],
                        [rq], "qn", "mq", 6)

        m1_proj(0)
        for ti in range(5):
            m1_evac(ti)
            if ti + 1 < 5:
                m1_proj(ti + 1)
            m1_chain(ti)
        self.dump("kvn", kvn[:], ["kvn"])
        self.dump("qn", qn[:], ["qn"])
        self.dump("krr", kT[0][64:96, :], ["kT0"])

        self.P.barrier()
        self.off = m1
        qT = [self.sb(f"qT{i}", [96, T], BF16) for i in range(2)]
        va = [self.sb(f"va{i}", [128, 18, 128], BF16) for i in range(2)]
        ex = [self.sb(f"ex{i}", [128, 512], BF16) for i in range(3)]
        rden = rtmp[1][0:64, :]
        for i in range(2):
            self.memset(va[i][:, :, 64:128], 1.0, [f"va{i}"])

        def expand_items(h):
            hb = h % 2
            kTk, qTk, vak = f"kT{hb}", f"qT{hb}", f"va{hb}"
            items = []
            for ti, (t0, tn) in enumerate(TCH):
                def f(ti=ti, t0=t0, tn=tn):
                    self.mm(self.psv(5)[0:64, 0:tn], wukv[:, h * 128:h * 128 + 64], kvn[:, t0:t0 + tn], True, True,
                            ["m_wukv", "kvn"], [("ps", 5)])
                    self.cp(kT[hb][0:64, t0:t0 + tn], self.psv(5)[0:64, 0:tn], [("ps", 5)], [kTk], eng="dve")
                    for s_ in range(2):
                        for kc in range(2):
                            self.mm(self.psv(6 + s_)[0:96, 0:tn], wuq[:, kc, s_, h * 96:(h + 1) * 96], qn[:, kc, t0:t0 + tn],
                                    kc == 0, kc == 1, ["m_wuq", "m_wuq2", "qn"], [("ps", 6 + s_)])
                    self.cp(qT[hb][0:64, t0:t0 + tn], self.psv(6)[0:64, 0:tn], [("ps", 6)], [qTk], eng="dve")
                    if ti < 4:
                        self.tt(rtmp[0][64:96, 0:tn], self.psv(6)[64:96, 0:tn], rope[64:96, 0, t0:t0 + tn], ALU.mult,
                                [("ps", 6), "rope"], ["m_rt0"])
                        self.tt(rtmp[1][64:96, 0:tn], self.psv(7)[64:96, 0:tn], rope[64:96, 1, t0:t0 + tn], ALU.mult,
                                [("ps", 7), "rope"], ["m_rt1"])
                        self.tt(qT[hb][64:96, t0:t0 + tn], rtmp[0][64:96, 0:tn], rtmp[1][64:96, 0:tn], ALU.add,
                                ["m_rt0", "m_rt1"], [qTk])
                    else:
                        self.cp(qT[hb][64:96, t0:t0 + tn], self.psv(6)[64:96, 0:tn], [("ps", 6), ("ps", 7)], [qTk], eng="dve")
                items.append(f)

            def fv():
                for kt in range(18):
                    bank = 5 + kt // 8
                    self.mm(self.psv(bank)[:, (kt % 8) * 64:(kt % 8) * 64 + 64], kvn[:, kt * 128:(kt + 1) * 128],
                            wukv[:, h * 128 + 64:h * 128 + 128], True, True, ["kvn", "m_wukv"], [("ps", bank)],
                            signal=(kt % 8 == 7 or kt == 17))
                for bank in range(5, 8):
                    n = 8 if bank < 7 else 2
                    self.cp(va[hb][:, (bank - 5) * 8:(bank - 5) * 8 + n, 0:64],
                            self.psv(bank)[:, 0:n * 64].rearrange("p (a b) -> p a b", b=64), [("ps", bank)], [vak], eng="dve")
            items.append(fv)
            return items

        for f in expand_items(0):
            f()
        self.dump("kT0", kT[0][:], ["kT0"])
        self.dump("qT0", qT[0][:], ["qT0"])
        self.dump("va0", va[0][:], ["va0"])
        LA = 2
        gi = 0
        for h in range(8):
            hb = h % 2
            kTk, qTk, vak = f"kT{hb}", f"qT{hb}", f"va{hb}"
            pend = expand_items(h + 1) if h + 1 < 8 else []
            its = []
            for qi, (q0, qn_) in enumerate(TCH):
                if qi == 4 and l == DEPTH - 1:
                    continue
                kts = list(range(18)) if qi < 4 else [16, 17]
                for n_, kt in enumerate(kts):
                    its.append((qi, q0, qn_, n_, kt, len(kts)))
            n_it = len(its)
            every = max(1, n_it // (len(pend) + 1)) if pend else 0

            def S(i):
                qi, q0, qn_, n_, kt, nk = its[i]
                b3 = (gi + i) % 3
                self.mm(self.psv(b3)[:, 0:qn_], kT[hb][:, kt * 128:(kt + 1) * 128], qT[hb][:, q0:q0 + qn_],
                        True, True, [kTk, qTk], [("ps", b3)])
                self.act(ex[b3][:, 0:qn_], self.psv(b3)[:, 0:qn_], AF.Exp, [("ps", b3)], [f"ex{b3}"], scale=ATTN_SCALE)

            def PV(i):
                qi, q0, qn_, n_, kt, nk = its[i]
                b3 = (gi + i) % 3
                po = 3 + (qi % 2)
                self.mm(self.psv(po)[:, 0:qn_], va[hb][:, kt, :], ex[b3][:, 0:qn_], n_ == 0, n_ == nk - 1,
                        [vak, f"ex{b3}"], [("ps", po)])
                if n_ == nk - 1:
                    self.P.op("dve", lambda e: e.reciprocal(rden[:, 0:qn_], self.psv(po)[64:128, 0:qn_]),
                              reads=[("ps", po)], writes=["m_rt1"])
                    self.tt(brT[(h % 2) * 64:(h % 2) * 64 + 64, h // 2, q0:q0 + qn_], self.psv(po)[0:64, 0:qn_],
                            rden[:, 0:qn_], ALU.mult, [("ps", po), "m_rt1"], ["brT"])

            for i in range(n_it + LA):
                if i < n_it:
                    S(i)
                if i >= LA:
                    PV(i - LA)
                if pend and i % every == every - 1:
                    pend.pop(0)()
            while pend:
                pend.pop(0)()
            gi += n_it
        self.dump("aT", brT[:], ["brT"])
        self.P.barrier()
        self.off = mm_
        self.tail(l, 0, brT, "mla_w_o", "first")
        self.off = m

    def gmlp(self, l):
        P, io = self.P, self.io
        m = self.off
        w_in = io["w_in"][l]
        brT = self.sb("gT", [128, 4, T], BF16)
        mm_ = self.off
        self.sel = self.sb("sel", [2, 128], F32)
        P.dma("sp", self.sel[:], io["c_sel"], writes=["sel"])
        wuv = self.sb("g_wuv", [128, 8, 1024], BF16)
        wsT = self.sb("g_wsT", [128, 8, 128], BF16)
        wsr = self.sb("g_wsr", [128, 128], F32)
        bsr = self.sb("g_bsr", [2, 4, 128], F32)
        bsb = self.sb("g_bsb", [128, 4, 128], F32)
        uT = self.sb("g_uT", [128, 4, 512], BF16)
        v32 = self.sb("g_v32", [128, 4, 512], F32)
        vb = self.sb("g_vb", [128, 4, 512], BF16)
        vsq = self.sb("g_vsq", [128, 4, 512], BF16)
        mean = self.sb("g_mean", [128, 512], F32)
        rstd = self.sb("g_rstd", [128, 512], F32)
        vc = self.sb("g_vc", [128, 512], F32)
        vn = self.sb("g_vn", [128, 4, 512], BF16)
        vtm = [self.sb(f"g_vtm{i}", [128, 512], BF16) for i in range(2)]
        mx = [self.sb(f"g_mx{i}", [128, 4, 128], F32) for i in range(2)]
        P.dma("pool", wuv[:], w_in[:, OFF_UV:OFF_UV + 1024].rearrange("(kc p) c -> p kc c", p=128), writes=["g_wuv"])
        for g in range(8):
            P.dma("sp", wsr[:], io["gm_w_s"][l, g], writes=["g_wsr"])
            self.tr(self.psv(0)[:, 0:128], wsr[:], self.ident[:], ["g_wsr", "ident"], [("ps", 0)])
            self.cp(wsT[:, g, :], self.psv(0)[:, 0:128], [("ps", 0)], ["g_wsT"], eng="dve")
        P.dma("sp", bsr[:], io["gm_b_s"][l].rearrange("(pair k) i -> k pair i", k=2), writes=["g_bsr"])
        self.mm(self.psv(1)[:, 0:512], self.sel[:], bsr[:].rearrange("k a i -> k (a i)"), True, True,
                ["sel", "g_bsr"], [("ps", 1)])
        self.cp(bsb[:], self.psv(1)[:, 0:512].rearrange("p (a i) -> p a i", i=128), [("ps", 1)], ["g_bsb"], eng="dve")
        inv = 1.0 / 512
        uT2 = [uT, self.sb("g_uT2", [128, 4, 512], BF16)]
        v322 = [v32, self.sb("g_v322", [128, 4, 512], F32)]
        tis = [ti for ti in range(5) if not (ti == 4 and l == DEPTH - 1)]

        def g_proj(ti):
            t0, tn = TCH[ti]
            b = ti % 2
            hk = ("hn", ti)
            for oc in range(8):
                pb = oc % 2
                for kc in range(8):
                    self.mm(self.psv(pb)[:, 0:tn], wuv[:, kc, oc * 128:(oc + 1) * 128], self.hn[:, kc, t0:t0 + tn],
                            kc == 0, kc == 7, ["g_wuv", hk], [("ps", pb)])
                if oc < 4:
                    self.act(uT2[b][:, oc, 0:tn], self.psv(pb)[:, 0:tn], AF.Gelu, [("ps", pb)], [f"g_uT{b}"])
                else:
                    self.act(v322[b][:, oc - 4, 0:tn], self.psv(pb)[:, 0:tn], AF.Gelu, [("ps", pb)], [f"g_v32{b}"])

        def g_mix(ti):
            t0, tn = TCH[ti]
            b = ti % 2
            uT_, v32_ = uT2[b], v322[b]
            uk, vk = f"g_uT{b}", f"g_v32{b}"
            self.cp(vb[:, :, 0:tn], v32_[:, :, 0:tn], [vk], ["g_vb"], eng="pool")
            self.act(vsq[:, :, 0:tn], v32_[:, :, 0:tn], AF.Square, [vk], ["g_vsq"])
            for k in range(4):
                self.mm(self.psv(2)[:, 0:tn], self.ones_b[:], vb[:, k, 0:tn], k == 0, k == 3, ["ones_b", "g_vb"], [("ps", 2)])
            for k in range(4):
                self.mm(self.psv(3)[:, 0:tn], self.ones_b[:], vsq[:, k, 0:tn], k == 0, k == 3, ["ones_b", "g_vsq"], [("ps", 3)])
            self.ts(mean[:, 0:tn], self.psv(2)[:, 0:tn], inv, None, ALU.mult, None, [("ps", 2)], ["g_mean"])
            self.tt(vc[:, 0:tn], mean[:, 0:tn], mean[:, 0:tn], ALU.mult, ["g_mean"], ["g_vc"])
            self.stt(rstd[:, 0:tn], self.psv(3)[:, 0:tn], inv, vc[:, 0:tn], ALU.mult, ALU.subtract,
                     [("ps", 3), "g_vc"], ["g_rstd"])
            self.act(rstd[:, 0:tn], rstd[:, 0:tn], AF.Sqrt, ["g_rstd", "eps_t"], ["g_rstd"], bias=self.eps_t[:, 0:1])
            self.P.op("dve", lambda e: e.reciprocal(rstd[:, 0:tn], rstd[:, 0:tn]), reads=["g_rstd"], writes=["g_rstd"])
            for k in range(4):
                self.tt(vc[:, 0:tn], v32_[:, k, 0:tn], mean[:, 0:tn], ALU.subtract, [vk, "g_mean"], ["g_vc"])
                self.stt(vn[:, k, 0:tn], vc[:, 0:tn], self.colsB[:, 4 + k:5 + k], rstd[:, 0:tn], ALU.mult, ALU.mult,
                         ["g_vc", "colsB", "g_rstd"], ["g_vn"])
            for c in range(tn // 128):
                cb = c % 2
                pbt = self.psv(4, BF16)
                for k in range(4):
                    self.tr(pbt[:, k * 128:(k + 1) * 128], vn[:, k, c * 128:(c + 1) * 128], self.identb[:],
                            ["g_vn", "identb"], [("ps", 4)], signal=(k == 3))
                self.cp(vtm[cb][:], pbt[:, 0:512], [("ps", 4)], [f"g_vtm{cb}"], eng="act")
                for g in range(8):
                    pair = g // 2
                    pm = 5 + (g % 2)
                    self.mm(self.psv(pm)[:, pair * 128:(pair + 1) * 128], vtm[cb][:, pair * 128:(pair + 1) * 128],
                            wsT[:, g, :], True, True, [f"g_vtm{cb}", "g_wsT"], [("ps", pm)], signal=(g >= 6))
                self.tt(mx[cb][0:64], self.psv(5)[0:64, :].rearrange("p (a i) -> p a i", i=128), bsb[0:64], ALU.add,
                        [("ps", 5), "g_bsb"], [f"g_mx{cb}"])
                self.tt(mx[cb][64:128], self.psv(6)[64:128, :].rearrange("p (a i) -> p a i", i=128), bsb[64:128], ALU.add,
                        [("ps", 6), "g_bsb"], [f"g_mx{cb}"])
                self.tt(brT[:, :, t0 + c * 128:t0 + (c + 1) * 128], mx[cb][:], uT_[:, :, c * 128:(c + 1) * 128], ALU.mult,
                        [f"g_mx{cb}", uk], ["brT"], eng="pool")

        g_proj(tis[0])
        for n_, ti in enumerate(tis):
            if n_ + 1 < len(tis):
                g_proj(tis[n_ + 1])
            g_mix(ti)
        self.dump("gT", brT[:], ["brT"])
        self.P.barrier()
        self.off = mm_
        self.tail(l, 2, brT, "gm_w_o", "mid")
        self.off = m

    def ssd(self, l):
        P, io = self.P, self.io
        m = self.off
        w_in = io["w_in"][l]
        brT = self.sb("sT", [128, 4, T], BF16)
        mm_ = self.off
        dt_all = self.sb("dt_all", [128, 18, 16], F32)
        a_all = self.sb("a_all", [128, 18, 16], F32)
        ma = self.off
        wx = self.sb("s_wx", [128, 8, 1024], BF16)
        wdt = self.sb("s_wdt", [128, 8, 16], BF16)
        rawt = [self.sb(f"s_raw{i}", [128, 2312], F32) for i in range(2)]
        acc = [self.sb(f"s_acc{i}", [128, 2308], F32) for i in range(2)]
        xst = [self.sb(f"s_xst{i}", [128, T], BF16) for i in range(2)]
        P.dma("pool", wx[:], w_in[:, 160:1184].rearrange("(kc p) c -> p kc c", p=128), writes=["s_wx"])
        P.dma("pool", wdt[:], w_in[:, 1184:1200].rearrange("(kc p) c -> p kc c", p=128), writes=["s_wdt"])
        for i in range(2):
            self.memset(rawt[i][:], 0.0, [f"s_raw{i}"])
        for cc in range(8):
            b = cc % 2
            rk, ak, xk = f"s_raw{b}", f"s_acc{b}", f"s_xst{b}"
            for ti, (t0, tn) in enumerate(TCH):
                pb = ti % 4
                for kc in range(8):
                    self.mm(self.psv(pb)[:, 0:tn], wx[:, kc, cc * 128:(cc + 1) * 128], self.hn[:, kc, t0:t0 + tn],
                            kc == 0, kc == 7, ["s_wx", ("hn", ti)], [("ps", pb)])
                c0 = 2 + t0 if ti < 4 else 2054
                self.cp(rawt[b][:, c0:c0 + tn], self.psv(pb)[:, 0:tn], [("ps", pb)], [rk], eng="act")
            self.ts(acc[b][:], rawt[b][:, 0:2308], self.convw[:, cc, 0:1], self.colsA[:, 120 + cc:121 + cc], ALU.mult, ALU.add,
                    [rk, "convw", "colsA"], [ak])
            for k in range(1, 5):
                self.stt(acc[b][:], rawt[b][:, k:k + 2308], self.convw[:, cc, k:k + 1], acc[b][:], ALU.mult, ALU.add,
                         [rk, "convw", ak], [ak])
            self.act(xst[b][:, 0:TL], acc[b][:, 0:TL], AF.Silu, [ak], [xk])
            self.act(xst[b][:, TL:T], acc[b][:, 2052:2308], AF.Silu, [ak], [xk])
            P.dma("sp", self.xbc_scr[cc], xst[b][:], reads=[xk], writes=["xbc_scr"], key="xbc_st")
        for tt_ in range(18):
            ti = min(tt_ // 4, 4)
            pb = 4 + tt_ % 2
            for kc in range(8):
                self.mm(self.psv(pb)[:, 0:16], self.hn[:, kc, tt_ * 128:(tt_ + 1) * 128], wdt[:, kc, :], kc == 0, kc == 7,
                        [("hn", ti), "s_wdt"], [("ps", pb)])
            self.tt(dt_all[:, tt_, :], self.psv(pb)[:, 0:16], self.ssm_bc[:, 16:32], ALU.add, [("ps", pb), "ssm_bc"], ["dt_all"])
        self.act(dt_all[:], dt_all[:], AF.Exp, ["dt_all"], ["dt_all"])
        self.act(dt_all[:], dt_all[:], AF.Ln, ["dt_all"], ["dt_all"], bias=self.ones_f[:, 0:1])
        self.stt(a_all[:], dt_all[:], -1.0, self.expA[:].unsqueeze(1).to_broadcast([128, 18, 16]), ALU.mult, ALU.mult,
                 ["dt_all", "expA"], ["a_all"])
        self.dump("dt_all", dt_all[:], ["dt_all"])
        self.P.barrier()
        self.off = ma
        wz = self.sb("s_wz", [128, 8, 512], BF16)
        P.dma("pool", wz[:], w_in[:, OFF_Z:OFF_Z + 512].rearrange("(kc p) c -> p kc c", p=128), writes=["s_wz"])
        st = self.sb("s_st", [128, 512], F32)
        hpb = self.sb("s_hpb", [128, 512], BF16)
        xcb = [self.sb(f"s_xc{i}", [128, 8, 128], BF16) for i in range(3)]
        NB = 2

        def mk(name, shape, dt):
            return [self.sb(f"{name}{i}", shape, dt) for i in range(NB)]
        xtm = mk("s_xtm", [128, 512], BF16)
        xdt = mk("s_xdt", [128, 512], BF16)
        xw = mk("s_xw", [128, 512], BF16)
        btm = mk("s_btm", [128, 256], BF16)
        E = mk("s_E", [128, 24], F32)
        rhsm = mk("s_rhsm", [128, 8, 128], F32)
        dec = mk("s_dec", [128, 8, 128], BF16)
        cbm = mk("s_cbm", [128, 2, 128], BF16)
        scT = mk("s_scT", [128, 8, 128], BF16)
        y1 = mk("s_y1", [128, 512], F32)
        y2 = mk("s_y2", [128, 512], F32)
        yfb = mk("s_yfb", [128, 512], BF16)
        zs = mk("s_zs", [128, 512], F32)
        ssq = mk("s_ssq", [128, 1], F32)
        ynb = mk("s_ynb", [128, 512], BF16)
        dbc = self.ssm_bc[:, 32:40]
        order_f = [16, 17] + list(range(16))
        order_b = [17, 16] + list(range(15, -1, -1))
        seq = [(0, ci) for ci in order_f] + [(1, ci) for ci in order_b]
        bc = lambda ap: ap.unsqueeze(2).to_broadcast([128, 8, 64])
        v3 = lambda t_: t_[:].rearrange("p (h q) -> p h q", q=64)
        xbc_v = self.xbc_scr.rearrange("cc p t -> p cc t")

        def load(n):
            d_, ci = seq[n]
            tok = ci * 128
            xb = n % 3
            P.dma("sp", xcb[xb][:], xbc_v[:, :, tok:tok + 128], reads=["xbc_scr"], writes=[f"s_xc{xb}"],
                  allow_slow_non_contiguous=True)

        def stageA(n):
            d_, ci = seq[n]
            s_ = n % NB
            xb = n % 3
            xk = f"s_xc{xb}"
            K = lambda nm: f"{nm}{s_}"
            tok = ci * 128
            ti = min(ci // 4, 4)
            mLE, mGT = (0, 1) if d_ == 0 else (2, 3)
            a_c = a_all[:, ci, d_ * 8:(d_ + 1) * 8]
            dt_c = dt_all[:, ci, d_ * 8:(d_ + 1) * 8]
            xc = xcb[xb]
            if d_ == 1:
                P.dma("act", yfb[s_][:], self.yf_scr[ci], reads=[("yf", ci)], writes=[K("s_yfb")], key=f"yf_ld{s_}")
            self.tt(rhsm[s_][:], a_c.unsqueeze(2).to_broadcast([128, 8, 128]),
                    self.masks[:, mLE, :].unsqueeze(1).to_broadcast([128, 8, 128]), ALU.mult,
                    ["a_all", "masks"], [K("s_rhsm")], eng="pool")
            p0 = self.psv(0, BF16)
            for k in range(4):
                self.tr(p0[:, k * 128:(k + 1) * 128], xc[:, k, :], self.identb[:], [xk, "identb"], [("ps", 0)], signal=False)
            for k in range(2):
                self.tr(p0[:, 512 + k * 128:512 + (k + 1) * 128], xc[:, 4 + k, :], self.identb[:], [xk, "identb"],
                        [("ps", 0)], signal=(k == 1))
            self.cp(xtm[s_][:], p0[:, 0:512], [("ps", 0)], [K("s_xtm")], eng="act")
            self.cp(btm[s_][:], p0[:, 512:768], [("ps", 0)], [K("s_btm")], eng="act")
            self.mm(self.psv(1)[:, 0:8], self.masks[:, mLE, :], a_c, True, True, ["masks", "a_all"], [("ps", 1)], signal=False)
            self.mm(self.psv(1)[:, 8:16], self.masks[:, mGT, :], a_c, True, True, ["masks", "a_all"], [("ps", 1)], signal=False)
            self.mm(self.psv(1)[:, 16:24], self.ones_f[:], a_c, True, True, ["ones_f", "a_all"], [("ps", 1)])
            self.act(E[s_][:], self.psv(1)[:, 0:24], AF.Exp, [("ps", 1)], [K("s_E")])
            self.tt(v3(xdt[s_]), v3(xtm[s_]), bc(dt_c), ALU.mult, [K("s_xtm"), "dt_all"], [K("s_xdt")], eng="pool")
            self.tt(v3(xw[s_]), v3(xdt[s_]), bc(E[s_][:, 8:16]), ALU.mult, [K("s_xdt"), K("s_E")], [K("s_xw")], eng="pool")
            if d_ == 1:
                self.tt(v3(y2[s_]), v3(xtm[s_]), bc(dbc), ALU.mult, [K("s_xtm"), "ssm_bc"], [K("s_y2")], eng="pool")
            for g in range(2):
                self.mm(self.psv(6)[:, g * 128:(g + 1) * 128], xc[:, 4 + g, :], xc[:, 6 + g, :],
                        True, True, [xk], [("ps", 6)], signal=(g == 1))
            self.tt(cbm[s_][:], self.psv(6)[:, 0:256].rearrange("p (g i) -> p g i", i=128),
                    self.masks_b[:, d_, :].unsqueeze(1).to_broadcast([128, 2, 128]), ALU.mult,
                    [("ps", 6), "masks_b"], [K("s_cbm")])
            for hf in range(2):
                self.mm(self.psv(4), self.masks[:, mGT, :], rhsm[s_][:, hf * 4:(hf + 1) * 4, :].rearrange("p a b -> p (a b)"),
                        True, True, ["masks", K("s_rhsm")], [("ps", 4)])
                self.act(dec[s_][:, hf * 4:(hf + 1) * 4, :].rearrange("p a b -> p (a b)"), self.psv(4), AF.Exp,
                         [("ps", 4)], [K("s_dec")])
                self.tt(scT[s_][:, hf * 4:(hf + 1) * 4, :], dec[s_][:, hf * 4:(hf + 1) * 4, :],
                        cbm[s_][:, hf, :].unsqueeze(1).to_broadcast([128, 4, 128]), ALU.mult,
                        [K("s_dec"), K("s_cbm")], [K("s_scT")])
            if d_ == 1:
                for kc in range(8):
                    self.mm(self.psv(5), self.hn[:, kc, tok:tok + 128], wz[:, kc, :], kc == 0, kc == 7,
                            [("hn", ti), "s_wz"], [("ps", 5)])
                self.act(zs[s_][:], self.psv(5), AF.Silu, [("ps", 5)], [K("s_zs")])

        def stageB(n):
            d_, ci = seq[n]
            s_ = n % NB
            xb = n % 3
            xk = f"s_xc{xb}"
            K = lambda nm: f"{nm}{s_}"
            xc = xcb[xb]
            if ci == 16 + d_ and n in (0, 18):
                self.memset(st[:], 0.0, ["s_st"])
                self.memset(hpb[:], 0.0, ["s_hpb"])
            for g in range(2):
                self.mm(self.psv(2)[:, g * 256:(g + 1) * 256], btm[s_][:, g * 128:(g + 1) * 128],
                        xw[s_][:, g * 256:(g + 1) * 256], True, True, [K("s_btm"), K("s_xw")], [("ps", 2)], signal=(g == 1))
            for g in range(2):
                self.mm(self.psv(3)[:, g * 256:(g + 1) * 256], xc[:, 6 + g, :], hpb[:, g * 256:(g + 1) * 256],
                        True, True, [xk, "s_hpb"], [("ps", 3)], signal=(g == 1))
            for h in range(8):
                self.mm(self.psv(7)[:, h * 64:(h + 1) * 64], scT[s_][:, h, :], xdt[s_][:, h * 64:(h + 1) * 64], True, True,
                        [K("s_scT"), K("s_xdt")], [("ps", 7)], signal=(h == 7))
            self.tt(v3(st), v3(st), bc(E[s_][:, 16:24]), ALU.mult, ["s_st", K("s_E")], ["s_st"])
            self.tt(st[:], st[:], self.psv(2), ALU.add, ["s_st", ("ps", 2)], ["s_st"])
            self.cp(hpb[:], st[:], ["s_st"], ["s_hpb"], eng="act")
            self.tt(v3(y1[s_]), self.psv(3).rearrange("p (h q) -> p h q", q=64), bc(E[s_][:, 0:8]), ALU.mult,
                    [("ps", 3), K("s_E")], [K("s_y1")])
            self.tt(y1[s_][:], y1[s_][:], self.psv(7), ALU.add, [K("s_y1"), ("ps", 7)], [K("s_y1")])
            if d_ == 0:
                self.cp(yfb[s_][:], y1[s_][:], [K("s_y1")], [K("s_yfb")], eng="act")
                P.dma("sp", self.yf_scr[ci], yfb[s_][:], reads=[K("s_yfb")], writes=[("yf", ci)], key="yf_st")
            else:
                self.tt(y1[s_][:], y1[s_][:], yfb[s_][:], ALU.add, [K("s_y1"), K("s_yfb")], [K("s_y1")])
                self.tt(y1[s_][:], y1[s_][:], y2[s_][:], ALU.add, [K("s_y1"), K("s_y2")], [K("s_y1")])
                self.tt(y1[s_][:], y1[s_][:], zs[s_][:], ALU.mult, [K("s_y1"), K("s_zs")], [K("s_y1")])
                self.P.op("act", lambda e: e.activation(y2[s_][:], y1[s_][:], AF.Square, accum_out=ssq[s_][:]),
                          reads=[K("s_y1")], writes=[K("s_y2"), K("s_ssq")])
                self.act(ssq[s_][:], ssq[s_][:], AF.Sqrt, [K("s_ssq"), "eps_t"], [K("s_ssq")], bias=self.eps_t[:, 0:1],
                         scale=1.0 / 512)
                self.P.op("dve", lambda e: e.reciprocal(ssq[s_][:], ssq[s_][:]), reads=[K("s_ssq")], writes=[K("s_ssq")])
                self.ts(ynb[s_][:], y1[s_][:], ssq[s_][:, 0:1], None, ALU.mult, None, [K("s_y1"), K("s_ssq")], [K("s_ynb")])

        def stageC(n):
            d_, ci = seq[n]
            if d_ == 0:
                return
            s_ = n % NB
            K = lambda nm: f"{nm}{s_}"
            tok = ci * 128
            p7 = self.psv(7, BF16)
            for k in range(4):
                self.tr(p7[:, k * 128:(k + 1) * 128], ynb[s_][:, k * 128:(k + 1) * 128], self.identb[:],
                        [K("s_ynb"), "identb"], [("ps", 7)], signal=(k == 3))
            for k in range(4):
                self.ts(brT[:, k, tok:tok + 128], p7[:, k * 128:(k + 1) * 128], self.colsB[:, k:k + 1], None,
                        ALU.mult, None, [("ps", 7), "colsB"], ["brT"])

        load(0)
        load(1)
        N = len(seq)
        for n in range(N + 2):
            if n < N:
                stageA(n)
            if 1 <= n <= N:
                stageB(n - 1)
            if n + 2 < N:
                load(n + 2)
            if n >= 2:
                stageC(n - 2)
        self.dump("sT", brT[:], ["brT"])
        self.P.barrier()
        self.off = mm_
        self.tail(l, 1, brT, "ssm_w_o", "last")
        self.off = m

    def layer(self, l):
        self.set_layer(l)
        self.dump(f"mod{l}", self.mod[:], ["mod"])
        if self.stop_after == f"L{l}params":
            return
        self.ffn(l, 1)
        self.dump(f"xres_ffn1_{l}", self.xres[:], [("xres", i) for i in range(5)])
        if self.stop_after == f"L{l}ffn1":
            return
        self.prenorm(1)
        self.dump(f"hn_mix_{l}", self.hn[:], [("hn", i) for i in range(5)])
        if self.stop_after == f"L{l}hn":
            return
        self.mla(l)
        self.dump(f"xres_mla_{l}", self.xres[:], [("xres", i) for i in range(5)])
        if self.stop_after == f"L{l}mla":
            return
        self.gmlp(l)
        self.dump(f"xres_gm_{l}", self.xres[:], [("xres", i) for i in range(5)])
        if self.stop_after == f"L{l}gm":
            return
        self.ssd(l)
        self.dump(f"xres_ssd_{l}", self.xres[:], [("xres", i) for i in range(5)])
        if self.stop_after == f"L{l}ssd":
            return
        self.ffn(l, 2, extra=(self.params_steps(l + 1) if l + 1 < DEPTH else None))
        self.dump(f"xres_ffn2_{l}", self.xres[:], [("xres", i) for i in range(5)])

    def final(self):
        P = self.P
        m = self.off
        sq = [self.sb(f"fsq{i}", [128, 8, 512], BF16) for i in range(2)]
        rs = [self.sb(f"frs{i}", [128, 512], F32) for i in range(2)]
        xn = [self.sb(f"fxn{i}", [128, 8, 512], F32) for i in range(2)]
        ot = [self.sb(f"fot{i}", [128, D], F32) for i in range(2)]
        oi = 0
        for ti in range(4):
            t0, tn = TCH[ti]
            b = ti % 2
            xk = ("xres", ti)
            self.act(sq[b][:], self.xres[:, :, t0:t0 + tn], AF.Square, [xk], [f"fsq{b}"])
            for dc in range(8):
                self.mm(self.psv(b), self.ones_b[:], sq[b][:, dc, :], dc == 0, dc == 7, ["ones_b", f"fsq{b}"], [("ps", b)])
            self.act(rs[b][:], self.psv(b), AF.Sqrt, [("ps", b), "eps_t"], [f"frs{b}"], bias=self.eps_t[:, 0:1], scale=1.0 / D)
            self.P.op("dve", lambda e, b=b: e.reciprocal(rs[b][:], rs[b][:]), reads=[f"frs{b}"], writes=[f"frs{b}"])
            for dc in range(8):
                self.stt(xn[b][:, dc, :], self.xres[:, dc, t0:t0 + tn], self.colsB[:, 11 + dc:12 + dc], rs[b][:],
                         ALU.mult, ALU.mult, [xk, "colsB", f"frs{b}"], [f"fxn{b}"])
            for c in range(4):
                ob = oi % 2
                oi += 1
                for half in range(2):
                    pb = 2 + (oi * 2 + half) % 4
                    for k in range(4):
                        dc = half * 4 + k
                        self.tr(self.psv(pb)[:, k * 128:(k + 1) * 128], xn[b][:, dc, c * 128:(c + 1) * 128], self.ident[:],
                                [f"fxn{b}", "ident"], [("ps", pb)], signal=(k == 3))
                    self.cp(ot[ob][:, half * 512:(half + 1) * 512], self.psv(pb), [("ps", pb)], [f"fot{ob}"])
                r0 = t0 + c * 128
                P.dma("sp", self.out[r0:r0 + 128, :], ot[ob][:], reads=[f"fot{ob}"], key="out")
        P.wait_all_dma("sp", ["out"])
        self.off = m


def build_program(dbg_specs=None, stop_after=None):
    nc = bass.Bass("TRN2", target_bir_lowering=False)
    stack = ExitStack()
    with stack:
        P = Prog(nc, stack)
        dbg = None
        if dbg_specs:
            dbg = {}
            for name, (shape, dt) in dbg_specs.items():
                dbg[name] = nc.dram_tensor("dbg_" + name, shape, dt, kind="ExternalOutput").ap()
        b = Builder(nc, P, dbg=dbg, stop_after=stop_after)
        b.build()
        P.emit()
    return nc


def _consts():
    ident = np.eye(128, dtype=np.float32)
    s = np.arange(128)[:, None]
    t = np.arange(128)[None, :]
    masks = np.stack([(s <= t), (s > t), (s >= t), (s < t)]).astype(np.float32)
    rows = TL // 64
    row = np.repeat(np.arange(rows, dtype=np.float32), 64)
    col = np.tile(np.arange(64, dtype=np.float32), rows)
    inv = np.power(np.float32(10000.0), -np.arange(8, dtype=np.float32) / np.float32(8)).astype(np.float32)
    ang = np.concatenate([row[:, None] * inv, col[:, None] * inv], axis=-1).astype(np.float32)
    cos = np.cos(ang).astype(np.float32).T
    sin = np.sin(ang).astype(np.float32).T
    cos2 = np.concatenate([cos, cos], axis=0)
    sin2s = np.concatenate([-sin, sin], axis=0)
    rope = np.stack([cos2, sin2s]).astype(np.float32)
    sel = np.zeros((2, 128), np.float32)
    sel[0, :64] = 1.0
    sel[1, 64:] = 1.0
    return ident, masks, rope, sel


def make_in_maps(inp, n_cores=8):
    f = lambda a: np.ascontiguousarray(np.asarray(a, dtype=np.float32))
    ident, masks, rope, sel = _consts()
    shared = {
        "w_ada": f(inp["w_ada"]), "b_ada": f(inp["b_ada"]).reshape(DEPTH, 72, 128),
        "norm_g": f(inp["norm_g"]).reshape(DEPTH, 24, 128),
        "ffn1_w_in": f(inp["ffn1_w_in"]), "ffn1_w_out": f(inp["ffn1_w_out"]),
        "ffn2_w_in": f(inp["ffn2_w_in"]), "ffn2_w_out": f(inp["ffn2_w_out"]),
        "w_in": f(inp["w_in"]), "mla_q_norm": f(inp["mla_q_norm"]).reshape(DEPTH, 2, 128),
        "mla_w_uq": f(inp["mla_w_uq"]), "mla_kv_norm": f(inp["mla_kv_norm"]).reshape(DEPTH, 1, 128),
        "mla_w_ukv": f(inp["mla_w_ukv"]), "mla_w_o": f(inp["mla_w_o"]),
        "ssm_conv_w": f(inp["ssm_conv_w"]), "ssm_conv_b": f(inp["ssm_conv_b"]).reshape(DEPTH, 8, 128),
        "ssm_small": np.ascontiguousarray(np.concatenate(
            [f(inp["ssm_a_log"]).reshape(DEPTH, 16), f(inp["ssm_dt_bias"]).reshape(DEPTH, 16),
             f(inp["ssm_d"]).reshape(DEPTH, 8)], axis=1).reshape(DEPTH, 1, 40)),
        "ssm_norm": f(inp["ssm_norm"]).reshape(DEPTH, 4, 128), "ssm_w_o": f(inp["ssm_w_o"]),
        "gm_norm": f(inp["gm_norm"]).reshape(DEPTH, 4, 128), "gm_w_s": f(inp["gm_w_s"]),
        "gm_b_s": f(inp["gm_b_s"]), "gm_w_o": f(inp["gm_w_o"]),
        "b_gate": f(inp["b_gate"]).reshape(DEPTH, 24, 128), "w_out": f(inp["w_out"]),
        "final_norm": f(inp["final_norm"]).reshape(8, 128),
        "c_ident": ident, "c_masks": masks, "c_rope": rope, "c_sel": sel,
    }
    x = f(inp["x"])
    c = f(inp["c"])
    ctx = f(inp["ctx"])
    cctx = f(inp["c_ctx"]).reshape(8, 128)
    maps = []
    for b in range(n_cores):
        d = dict(shared)
        d["x"] = x[b]
        d["ctx"] = ctx[b]
        d["crow"] = np.ascontiguousarray(np.concatenate([c[b].reshape(8, 128), cctx], axis=0))
        maps.append(d)
    return maps


_NC_CACHE = {}


def kernel(**inputs):
    if "nc" not in _NC_CACHE:
        _NC_CACHE["nc"] = build_program()
    nc = _NC_CACHE["nc"]
    maps = make_in_maps(inputs, 8)
    res = run_bass_kernel_spmd(nc, maps, core_ids=list(range(8)))
    out = np.stack([np.asarray(res.results[b]["out"], dtype=np.float32) for b in range(8)], axis=0)
    return out
```

```python
import numpy as np
from contextlib import ExitStack
import concourse.bass as bass
import concourse.mybir as mybir
from concourse.bass_utils import run_bass_kernel_spmd

F32 = mybir.dt.float32
BF16 = mybir.dt.bfloat16
AF = mybir.ActivationFunctionType
ALU = mybir.AluOpType

ENGS = ("pe", "act", "dve", "pool", "sp")
EPOCH = 30000

D = 1024
TL = 2048
TCX = 256
T = TL + TCX
DEPTH = 2
DFF = 2816
DIN = 6064
KV_SIDE = 1200
EPS = 1e-6
ATTN_SCALE = 96 ** -0.5
TCH = [(0, 512), (512, 512), (1024, 512), (1536, 512), (2048, 256)]
OFF_Q = KV_SIDE
OFF_Z = KV_SIDE + 256
OFF_UV = OFF_Z + 512
OFF_G = OFF_UV + 1024


class Prog:
    def __init__(self, nc, stack):
        self.nc = nc
        self.stack = stack
        self.ops = {e: [] for e in ENGS}
        self.seen = {e: {} for e in ENGS}
        self.buf = {}
        self.esem = {e: [] for e in ENGS}
        self.dma_sem = {}

    def _need(self, eng, token, waits):
        if token is None:
            return
        if token[0] == "eng":
            _, f, idx = token
            if f == "pe" and eng == "pe":
                return
            k = ("eng", f)
            if self.seen[eng].get(k, -1) >= idx:
                return
            self.seen[eng][k] = idx
            waits.append(token)
        else:
            _, key, val = token
            k = ("dma", key)
            if self.seen[eng].get(k, -1) >= val:
                return
            self.seen[eng][k] = val
            waits.append(token)

    def _deps(self, eng, reads, writes):
        waits = []
        for k in reads:
            st = self.buf.get(k)
            if st is not None:
                self._need(eng, st[0], waits)
        for k in writes:
            st = self.buf.get(k)
            if st is not None:
                self._need(eng, st[0], waits)
                for t in st[1].values():
                    self._need(eng, t, waits)
        return waits

    def _commit(self, token, reads, writes):
        src = (token[0], token[1])
        for k in reads:
            st = self.buf.setdefault(k, [None, {}])
            st[1][src] = token
        for k in writes:
            self.buf[k] = [token, {}]

    def op(self, eng, fn, reads=(), writes=(), signal=True):
        waits = self._deps(eng, reads, writes)
        idx = len(self.ops[eng])
        self.ops[eng].append(["op", waits, fn, signal])
        self._commit(("eng", eng, idx), reads, writes)
        return idx

    def dma(self, eng, out_ap, in_ap, reads=(), writes=(), key=None, **kw):
        if key is None:
            key = writes[0] if writes else ("st", reads[0])
        waits = self._deps(eng, reads, writes)
        ent = self.dma_sem.get(key)
        if ent is None:
            ent = [self.stack.enter_context(self.nc.semaphore(f"s_dma{len(self.dma_sem)}")), 0]
            self.dma_sem[key] = ent
        ent[1] += 16
        val = ent[1]
        self.ops[eng].append(["dma", waits, (out_ap, in_ap, kw), (ent[0], val)])
        self._commit(("dma", key, val), reads, writes)

    def wait_all_dma(self, eng, keys):
        waits = []
        for key in keys:
            ent = self.dma_sem[key]
            self._need(eng, ("dma", key, ent[1]), waits)
        self.ops[eng].append(["wait", waits, None, False])

    def barrier(self):
        for e in ENGS:
            waits = []
            for f in ENGS:
                if f == e:
                    continue
                last = None
                for i in range(len(self.ops[f]) - 1, -1, -1):
                    if self.ops[f][i][0] == "op":
                        last = i
                        break
                if last is not None:
                    self.ops[f][last][3] = True
                    self._need(e, ("eng", f, last), waits)
            for key, ent in self.dma_sem.items():
                self._need(e, ("dma", key, ent[1]), waits)
            self.ops[e].append(["wait", waits, None, False])

    def emit(self):
        nc = self.nc
        signum = {}
        for e in ENGS:
            ops = self.ops[e]
            for i in range(len(ops) - 1, -1, -1):
                if ops[i][0] == "op":
                    ops[i][3] = True
                    break
            nums = [0] * len(ops)
            c = 0
            for i, o in enumerate(ops):
                if o[0] == "op" and o[3]:
                    c += 1
                nums[i] = c
            res = [None] * len(ops)
            nxt = None
            for i in range(len(ops) - 1, -1, -1):
                if ops[i][0] == "op" and ops[i][3]:
                    nxt = nums[i]
                res[i] = nxt
            signum[e] = (nums, res)
            nep = (c + EPOCH - 1) // EPOCH
            self.esem[e] = [self.stack.enter_context(nc.semaphore(f"s_{e}{i}")) for i in range(max(nep, 1))]

        def sem_for(e, n):
            return self.esem[e][(n - 1) // EPOCH], ((n - 1) % EPOCH) + 1

        def run_engine(e, eh):
            ops = self.ops[e]
            nums, _ = signum[e]
            for i, o in enumerate(ops):
                kind, waits = o[0], o[1]
                for t in waits:
                    if t[0] == "eng":
                        _, f, idx = t
                        n = signum[f][1][idx]
                        assert n is not None
                        s, v = sem_for(f, n)
                        eh.wait_ge(s, v)
                    else:
                        _, key, val = t
                        eh.wait_ge(self.dma_sem[key][0], val)
                if kind == "op":
                    ins = o[2](eh)
                    if o[3]:
                        s, v = sem_for(e, nums[i])
                        ins.then_inc(s, 1)
                elif kind == "dma":
                    out_ap, in_ap, kw = o[2]
                    s, v = o[3]
                    eh.dma_start(out=out_ap, in_=in_ap, **kw).then_inc(s, 16)

        with nc.Block() as block:
            @block.tensor
            def _(eh):
                run_engine("pe", eh)

            @block.scalar
            def _(eh):
                run_engine("act", eh)

            @block.vector
            def _(eh):
                run_engine("dve", eh)

            @block.gpsimd
            def _(eh):
                run_engine("pool", eh)

            @block.sync
            def _(eh):
                run_engine("sp", eh)


IN_SPECS = [
    ("x", [TL, D]), ("crow", [16, 128]), ("ctx", [TCX, D]),
    ("w_ada", [DEPTH, D, 9 * D]), ("b_ada", [DEPTH, 72, 128]), ("norm_g", [DEPTH, 24, 128]),
    ("ffn1_w_in", [DEPTH, D, 2 * DFF]), ("ffn1_w_out", [DEPTH, DFF, D]),
    ("ffn2_w_in", [DEPTH, D, 2 * DFF]), ("ffn2_w_out", [DEPTH, DFF, D]),
    ("w_in", [DEPTH, D, DIN]), ("mla_q_norm", [DEPTH, 2, 128]), ("mla_w_uq", [DEPTH, 256, 768]),
    ("mla_kv_norm", [DEPTH, 1, 128]), ("mla_w_ukv", [DEPTH, 128, 1024]), ("mla_w_o", [DEPTH, 512, D]),
    ("ssm_conv_w", [DEPTH, 1024, 5]), ("ssm_conv_b", [DEPTH, 8, 128]), ("ssm_small", [DEPTH, 1, 40]),
    ("ssm_norm", [DEPTH, 4, 128]), ("ssm_w_o", [DEPTH, 512, D]),
    ("gm_norm", [DEPTH, 4, 128]), ("gm_w_s", [DEPTH, 8, 128, 128]), ("gm_b_s", [DEPTH, 8, 128]),
    ("gm_w_o", [DEPTH, 512, D]), ("b_gate", [DEPTH, 24, 128]), ("w_out", [DEPTH, D, D]),
    ("final_norm", [8, 128]),
    ("c_ident", [128, 128]), ("c_masks", [4, 128, 128]), ("c_rope", [2, 32, TL]), ("c_sel", [2, 128]),
]


class Builder:
    def __init__(self, nc, P, dbg=None, stop_after=None):
        self.nc = nc
        self.P = P
        self.off = 16512
        self.top = 229344
        self.uid = 0
        self.dbg = dbg
        self.stop_after = stop_after
        self.io = {}
        for name, shape in IN_SPECS:
            self.io[name] = nc.dram_tensor(name, shape, F32, kind="ExternalInput").ap()
        self.out = nc.dram_tensor("out", [TL, D], F32, kind="ExternalOutput").ap()
        self.yf_scr = nc.dram_tensor("yf_scr", [18, 128, 512], BF16, kind="Internal").ap()
        self.xbc_scr = nc.dram_tensor("xbc_scr", [8, 128, T], BF16, kind="Internal").ap()
        self.m_scr = nc.dram_tensor("m_scr", [8, 128, T], BF16, kind="Internal").ap()
        self.ps = [nc.alloc_psum_tensor(f"psb{i}", [128, 512], F32) for i in range(8)]
        self.rr = 0

    def sb(self, name, shape, dtype):
        esz = 4 if dtype == F32 else 2
        n = 1
        for s in shape[1:]:
            n *= s
        nbytes = (n * esz + 31) // 32 * 32
        assert self.off + nbytes <= self.top, (name, self.off, nbytes)
        self.uid += 1
        t = self.nc.alloc_sbuf_tensor_at(f"{name}_{self.uid}", shape, dtype, offset=self.off)
        self.off += nbytes
        return t

    def psv(self, i, dtype=F32):
        ap = self.ps[i][:]
        if dtype != F32:
            ap = ap.bitcast(dtype)
        return ap

    def mm(self, out, lhsT, rhs, start, stop, reads, writes, signal=None):
        if signal is None:
            signal = stop
        self.P.op("pe", lambda e: e.matmul(out, lhsT, rhs, start=start, stop=stop),
                  reads=reads, writes=writes, signal=signal)

    def tr(self, out, in_, ident, reads, writes, signal=True):
        self.P.op("pe", lambda e: e.transpose(out, in_, ident), reads=reads, writes=writes, signal=signal)

    def act(self, out, in_, func, reads, writes, bias=None, scale=None):
        kw = {}
        if bias is not None:
            kw["bias"] = bias
        if scale is not None:
            kw["scale"] = scale
        self.P.op("act", lambda e: e.activation(out, in_, func, **kw), reads=reads, writes=writes)

    def tt(self, out, a, b, op, reads, writes, eng="dve"):
        self.P.op(eng, lambda e: e.tensor_tensor(out, a, b, op), reads=reads, writes=writes)

    def ts(self, out, a, s1, s2, op0, op1, reads, writes, eng="dve"):
        if s2 is None:
            self.P.op(eng, lambda e: e.tensor_scalar(out, a, s1, None, op0), reads=reads, writes=writes)
        else:
            self.P.op(eng, lambda e: e.tensor_scalar(out, a, s1, s2, op0, op1), reads=reads, writes=writes)

    def stt(self, out, a, s, b, op0, op1, reads, writes, eng="dve"):
        self.P.op(eng, lambda e: e.scalar_tensor_tensor(out, a, s, b, op0, op1), reads=reads, writes=writes)

    def cp(self, out, in_, reads, writes, eng=None):
        if eng is None:
            self.rr ^= 1
            eng = "dve" if self.rr else "act"
        if eng == "act":
            self.P.op("act", lambda e: e.copy(out, in_), reads=reads, writes=writes)
        else:
            self.P.op(eng, lambda e: e.tensor_copy(out, in_), reads=reads, writes=writes)

    def memset(self, ap, val, writes, eng="dve"):
        self.P.op(eng, lambda e: e.memset(ap, val), writes=writes)

    def dump(self, name, ap, reads):
        if self.dbg is None or name not in self.dbg:
            return
        d = self.dbg[name]
        self.P.dma("sp", d, ap, reads=reads, key="dbg")

    def build(self):
        nc, P, io = self.nc, self.P, self.io
        self.xres = self.sb("xres", [128, 8, T], F32)
        self.hn = self.sb("hn", [128, 8, T], BF16)
        self.ident = self.sb("ident", [128, 128], F32)
        self.identb = self.sb("identb", [128, 128], BF16)
        self.ones_f = self.sb("ones_f", [128, 128], F32)
        self.ones_b = self.sb("ones_b", [128, 128], BF16)
        self.masks = self.sb("masks", [128, 4, 128], F32)
        self.masks_b = self.sb("masks_b", [128, 2, 128], BF16)
        self.eps_t = self.sb("eps_t", [128, 1], F32)
        self.silc = self.sb("silc", [128, 16], F32)
        self.silcb = self.sb("silcb", [128, 16], BF16)
        self.LP = []
        for l in range(DEPTH):
            self.LP.append(dict(
                colsA=self.sb("colsA", [128, 128], F32),
                colsB=self.sb("colsB", [128, 32], F32),
                mod=self.sb("mod", [128, 72, 2], F32),
                nA=self.sb("nA", [128, 3, 8, 2], F32),
                gsc=self.sb("gsc", [128, 3, 8, 2], F32),
                ssm_bc=self.sb("ssm_bc", [128, 40], F32),
                expA=self.sb("expA", [128, 16], F32),
                convw=self.sb("convw", [128, 8, 5], F32)))
        self.persist_off = self.off

        P.dma("sp", self.ident[:], io["c_ident"], writes=["ident"])
        P.dma("sp", self.masks[:], io["c_masks"].rearrange("m s t -> s m t"), writes=["masks"])
        self.cp(self.identb[:], self.ident[:], ["ident"], ["identb"], eng="dve")
        self.memset(self.ones_f[:], 1.0, ["ones_f"])
        self.memset(self.ones_b[:], 1.0, ["ones_b"])
        self.memset(self.eps_t[:], EPS, ["eps_t"])
        self.cp(self.masks_b[:, 0, :], self.masks[:, 0, :], ["masks"], ["masks_b"], eng="dve")
        self.cp(self.masks_b[:, 1, :], self.masks[:, 2, :], ["masks"], ["masks_b"], eng="dve")

        mp = self.off
        self.wada = [self.sb(f"wada{i}", [128, 8, 1024], BF16) for i in range(3)]
        steps = self.params_steps(0)
        steps[0]()
        self.load_x(barrier=False)
        for f in steps[1:]:
            f()
        self.P.barrier()
        self.off = mp
        if self.stop_after == "load":
            return self.finish_dbg()
        for l in range(DEPTH):
            self.layer(l)
            if self.stop_after is not None and self.stop_after.startswith(f"L{l}"):
                return self.finish_dbg()
        self.final()
        return None

    def finish_dbg(self):
        if "dbg" in self.P.dma_sem:
            self.P.wait_all_dma("sp", ["dbg"])

    def load_x(self, barrier=True):
        P, io = self.P, self.io
        m = self.off
        stg = [self.sb(f"xstg{i}", [128, D], F32) for i in range(2)]
        for tt in range(18):
            s = stg[tt % 2]
            sk = f"xstg{tt % 2}"
            src = io["x"][tt * 128:(tt + 1) * 128, :] if tt < 16 else io["ctx"][(tt - 16) * 128:(tt - 15) * 128, :]
            P.dma("sp", s[:], src, writes=[sk])
            for half in range(2):
                pb = 4 + (tt * 2 + half) % 4
                for k in range(4):
                    dc = half * 4 + k
                    self.tr(self.psv(pb)[:, k * 128:(k + 1) * 128], s[:, dc * 128:(dc + 1) * 128], self.ident[:],
                            reads=[sk, "ident"], writes=[("ps", pb)], signal=(k == 3))
                self.cp(self.xres[:, half * 4:half * 4 + 4, tt * 128:(tt + 1) * 128],
                        self.psv(pb).rearrange("p (k t) -> p k t", k=4), [("ps", pb)], [("xres", tt // 4)], eng="dve")
        if barrier:
            self.P.barrier()
            self.off = m

    def set_layer(self, l):
        for k, v in self.LP[l].items():
            setattr(self, k, v)

    def params_steps(self, l):
        P, io = self.P, self.io
        lp = self.LP[l]
        colsA, colsB, mod, nA, gsc, ssm_bc, expA, convw = (lp[k] for k in
                                                          ("colsA", "colsB", "mod", "nA", "gsc", "ssm_bc", "expA", "convw"))
        K = lambda n: f"{n}{l}"
        st = {}

        def slab_dma(j):
            jj = l * 9 + j
            P.dma("pool", self.wada[jj % 3][:],
                  io["w_ada"][l, :, j * 1024:(j + 1) * 1024].rearrange("(kc p) c -> p kc c", p=128),
                  writes=[f"wada{jj % 3}"])

        def s0():
            rows = self.sb("rows", [128, 128], F32)
            rows2 = self.sb("rows2", [32, 128], F32)
            small = self.sb("small", [1, 40], F32)
            q = "act"
            P.dma(q, rows[0:72, :], io["b_ada"][l], writes=[K("rows")], key=K("rows"))
            P.dma(q, rows[72:96, :], io["norm_g"][l], writes=[K("rows")], key=K("rows"))
            P.dma(q, rows[96:120, :], io["b_gate"][l], writes=[K("rows")], key=K("rows"))
            P.dma(q, rows[120:128, :], io["ssm_conv_b"][l], writes=[K("rows")], key=K("rows"))
            P.dma(q, rows2[0:4, :], io["ssm_norm"][l], writes=[K("rows2")], key=K("rows2"))
            P.dma(q, rows2[4:8, :], io["gm_norm"][l], writes=[K("rows2")], key=K("rows2"))
            P.dma(q, rows2[8:10, :], io["mla_q_norm"][l], writes=[K("rows2")], key=K("rows2"))
            P.dma(q, rows2[10:11, :], io["mla_kv_norm"][l], writes=[K("rows2")], key=K("rows2"))
            P.dma(q, rows2[11:19, :], io["final_norm"], writes=[K("rows2")], key=K("rows2"))
            P.dma(q, rows2[19:27, :], io["crow"][0:8, :], writes=[K("rows2")], key=K("rows2"))
            P.dma(q, rows2[27:32, :], io["crow"][8:13, :], writes=[K("rows2")], key=K("rows2"))
            P.dma(q, small[:], io["ssm_small"][l], writes=[K("small")])
            P.dma(q, convw[:], io["ssm_conv_w"][l].rearrange("(cc p) k -> p cc k", p=128),
                  writes=[K("convw")], allow_slow_non_contiguous=True)
            slab_dma(0)
            slab_dma(1)
            if l == 0:
                crow = self.sb("crowt", [16, 128], F32)
                P.dma(q, crow[:], io["crow"], writes=["crowt"])
            self.tr(self.psv(0)[:, 0:128], rows[:], self.ident[:], [K("rows"), "ident"], [("ps", 0)])
            self.cp(colsA[:], self.psv(0)[:, 0:128], [("ps", 0)], [K("colsA")], eng="dve")
            self.tr(self.psv(1)[:, 0:32], rows2[:], self.ident[0:32, 0:32], [K("rows2"), "ident"], [("ps", 1)])
            self.cp(colsB[:], self.psv(1)[:, 0:32], [("ps", 1)], [K("colsB")], eng="dve")
            self.mm(self.psv(2)[:, 0:40], self.ones_f[0:1, :], small[:], True, True, ["ones_f", K("small")], [("ps", 2)])
            self.cp(ssm_bc[:], self.psv(2)[:, 0:40], [("ps", 2)], [K("ssm_bc")], eng="dve")
            self.act(expA[:], ssm_bc[:, 0:16], AF.Exp, [K("ssm_bc")], [K("expA")])
            if l == 0:
                self.tr(self.psv(3)[:, 0:16], crow[:], self.ident[0:16, 0:16], ["crowt", "ident"], [("ps", 3)])
                self.act(self.silc[:], self.psv(3)[:, 0:16], AF.Silu, [("ps", 3)], ["silc"])
                self.cp(self.silcb[:], self.silc[:], ["silc"], ["silcb"], eng="dve")

        def sj(j):
            if j + 2 < 9:
                slab_dma(j + 2)
            jj = l * 9 + j
            wb = self.wada[jj % 3]
            wk = f"wada{jj % 3}"
            silc3 = self.silcb[:].rearrange("p (w k) -> p w k", w=2)
            pb = 4 + (j % 2)
            for dc in range(8):
                for kc in range(8):
                    self.mm(self.psv(pb)[:, dc * 2:dc * 2 + 2], wb[:, kc, dc * 128:(dc + 1) * 128], silc3[:, :, kc],
                            kc == 0, kc == 7, [wk, "silcb"], [("ps", pb)], signal=(kc == 7 and dc == 7))
            self.tt(mod[:, j * 8:(j + 1) * 8, :], self.psv(pb)[:, 0:16].rearrange("p (a b) -> p a b", b=2),
                    colsA[:, j * 8:(j + 1) * 8].unsqueeze(2).to_broadcast([128, 8, 2]), ALU.add,
                    [("ps", pb), K("colsA")], [K("mod")])

        def sl():
            for j in range(3):
                g_bc = colsA[:, 72 + j * 8:72 + (j + 1) * 8].unsqueeze(2).to_broadcast([128, 8, 2])
                self.stt(nA[:, j], mod[:, (3 * j + 1) * 8:(3 * j + 2) * 8, :], 1.0, g_bc, ALU.add, ALU.mult,
                         [K("mod"), K("colsA")], [K("nA")])
                self.ts(gsc[:, j], mod[:, (3 * j + 2) * 8:(3 * j + 3) * 8, :], 1.0 if j == 1 else 0.5, None,
                        ALU.mult, None, [K("mod")], [K("gsc")])

        return [s0] + [(lambda j=j: sj(j)) for j in range(9)] + [sl]

    def prenorm(self, j, nti=5):
        P = self.P
        m = self.off
        sq = [self.sb(f"sq{i}", [128, 8, 512], BF16) for i in range(2)]
        rs = [self.sb(f"rs{i}", [128, 512], F32) for i in range(2)]
        tmp = [self.sb(f"ntmp{i}", [128, 512], F32) for i in range(2)]

        def stats(ti):
            t0, tn = TCH[ti]
            b = ti % 2
            xk = ("xres", ti)
            self.act(sq[b][:, :, 0:tn], self.xres[:, :, t0:t0 + tn], AF.Square, [xk], [f"sq{b}"])
            for dc in range(8):
                self.mm(self.psv(b)[:, 0:tn], self.ones_b[:], sq[b][:, dc, 0:tn], dc == 0, dc == 7,
                        ["ones_b", f"sq{b}"], [("ps", b)])
            self.act(rs[b][:, 0:tn], self.psv(b)[:, 0:tn], AF.Sqrt, [("ps", b), "eps_t"], [f"rs{b}"],
                     bias=self.eps_t[:, 0:1], scale=1.0 / D)
            self.P.op("dve", lambda e: e.reciprocal(rs[b][:, 0:tn], rs[b][:, 0:tn]),
                      reads=[f"rs{b}"], writes=[f"rs{b}"])

        def apply(ti):
            t0, tn = TCH[ti]
            who = 1 if ti == 4 else 0
            b = ti % 2
            xk = ("xres", ti)
            for dc in range(8):
                tb = dc % 2
                self.stt(tmp[tb][:, 0:tn], self.xres[:, dc, t0:t0 + tn], self.nA[:, j, dc, who:who + 1], rs[b][:, 0:tn],
                         ALU.mult, ALU.mult, [xk, "nA", f"rs{b}"], [f"ntmp{tb}"])
                if dc % 4 != 3:
                    self.act(self.hn[:, dc, t0:t0 + tn], tmp[tb][:, 0:tn], AF.Identity, [f"ntmp{tb}", "mod"], [("hn", ti)],
                             bias=self.mod[:, 3 * j * 8 + dc, who:who + 1])
                else:
                    self.ts(self.hn[:, dc, t0:t0 + tn], tmp[tb][:, 0:tn], self.mod[:, 3 * j * 8 + dc, who:who + 1], None,
                            ALU.add, None, [f"ntmp{tb}", "mod"], [("hn", ti)])

        stats(0)
        for ti in range(nti):
            if ti + 1 < nti:
                stats(ti + 1)
            apply(ti)
        self.P.barrier()
        self.off = m

    def ffn(self, l, which, extra=None):
        P, io = self.P, self.io
        j = 0 if which == 1 else 2
        w_in = io[f"ffn{which}_w_in"][l]
        w_out = io[f"ffn{which}_w_out"][l]
        self.prenorm(j, nti=(4 if (l == DEPTH - 1 and which == 2) else 5))
        m = self.off
        NG = 11
        wg = [self.sb(f"fwg{i}", [128, 8, 256], BF16) for i in range(2)]
        wu = [self.sb(f"fwu{i}", [128, 8, 256], BF16) for i in range(2)]
        wo = [self.sb(f"fwo{i}", [128, 2, 1024], BF16) for i in range(2)]
        sg = [self.sb(f"fsg{i}", [128, 512], F32) for i in range(2)]
        at = [self.sb(f"fat{i}", [128, 2, 512], BF16) for i in range(2)]
        extra = list(extra) if extra else []
        if extra:
            self.wada = [self.sb(f"wada{i}", [128, 8, 1024], BF16) for i in range(3)]

        def load(g):
            b = g % 2
            P.dma("pool", wg[b][:], w_in[:, g * 256:(g + 1) * 256].rearrange("(kc p) c -> p kc c", p=128),
                  writes=[f"fwg{b}"])
            P.dma("pool", wu[b][:], w_in[:, DFF + g * 256:DFF + (g + 1) * 256].rearrange("(kc p) c -> p kc c", p=128),
                  writes=[f"fwu{b}"])
            P.dma("pool", wo[b][:], w_out[g * 256:(g + 1) * 256, :].rearrange("(hc p) d -> p hc d", p=128),
                  writes=[f"fwo{b}"])

        load(0)
        nti = 4 if (l == DEPTH - 1 and which == 2) else 5
        iters = [(g, ti) for g in range(NG) for ti in range(nti)]

        def up(i):
            g, ti = iters[i]
            t0, tn = TCH[ti]
            b, ab = g % 2, i % 2
            for hc in range(2):
                pg, pu = 0 + hc, 2 + hc
                for kc in range(8):
                    self.mm(self.psv(pg)[:, 0:tn], wg[b][:, kc, hc * 128:(hc + 1) * 128], self.hn[:, kc, t0:t0 + tn],
                            kc == 0, kc == 7, [f"fwg{b}", ("hn", ti)], [("ps", pg)])
                for kc in range(8):
                    self.mm(self.psv(pu)[:, 0:tn], wu[b][:, kc, hc * 128:(hc + 1) * 128], self.hn[:, kc, t0:t0 + tn],
                            kc == 0, kc == 7, [f"fwu{b}", ("hn", ti)], [("ps", pu)])
                self.act(sg[hc][:, 0:tn], self.psv(pg)[:, 0:tn], AF.Silu, [("ps", pg)], [f"fsg{hc}"])
                self.tt(at[ab][:, hc, 0:tn], sg[hc][:, 0:tn], self.psv(pu)[:, 0:tn], ALU.mult,
                        [f"fsg{hc}", ("ps", pu)], [f"fat{ab}"])

        def down(i):
            g, ti = iters[i]
            t0, tn = TCH[ti]
            who = 1 if ti == 4 else 0
            b, ab = g % 2, i % 2
            for dc in range(8):
                po = 4 + (dc % 4)
                for hc in range(2):
                    self.mm(self.psv(po)[:, 0:tn], wo[b][:, hc, dc * 128:(dc + 1) * 128], at[ab][:, hc, 0:tn],
                            hc == 0, hc == 1, [f"fwo{b}", f"fat{ab}"], [("ps", po)])
                self.stt(self.xres[:, dc, t0:t0 + tn], self.psv(po)[:, 0:tn], self.gsc[:, j, dc, who:who + 1],
                         self.xres[:, dc, t0:t0 + tn], ALU.mult, ALU.add, [("ps", po), "gsc", ("xres", ti)],
                         [("xres", ti)])

        for i in range(len(iters) + 1):
            if i < len(iters):
                up(i)
            if i >= 1:
                down(i - 1)
            if i < len(iters) and iters[i][1] == 0 and iters[i][0] + 1 < NG:
                load(iters[i][0] + 1)
            if extra and i % 4 == 3:
                extra.pop(0)()
        while extra:
            extra.pop(0)()
        self.P.barrier()
        self.off = m

    def tail(self, l, br, brT, w_o_name, mode):
        P, io = self.P, self.io
        m = self.off
        wgt = self.sb("twg", [128, 8, 1024], BF16)
        wo = self.sb("two", [128, 4, 1024], BF16)
        sig = [self.sb(f"tsig{i}", [128, 512], F32) for i in range(2)]
        mt = [self.sb(f"tm{i}", [128, 8, 512], BF16) for i in range(2)]
        c0 = OFF_G + br * 1024
        for dc in range(8):
            P.dma("pool", wo[:, :, dc * 128:(dc + 1) * 128],
                  io[w_o_name][l][:, dc * 128:(dc + 1) * 128].rearrange("(kc p) c -> p kc c", p=128), writes=[("two", dc)])
            P.dma("pool", wgt[:, :, dc * 128:(dc + 1) * 128],
                  io["w_in"][l][:, c0 + dc * 128:c0 + (dc + 1) * 128].rearrange("(kc p) c -> p kc c", p=128),
                  writes=[("twg", dc)])
        if mode == "last":
            wout = self.sb("twout", [128, 8, 1024], BF16)
            P.dma("pool", wout[:], io["w_out"][l].rearrange("(kc p) c -> p kc c", p=128), writes=["twout"])
            mprev = [self.sb("tmp0", [128, 8, 512], BF16)] * 2
            mpk = ["tmp0", "tmp0"]
        elif mode == "mid":
            mprev = [self.sb(f"tmp{i}", [128, 8, 512], BF16) for i in range(2)]
            mpk = ["tmp0", "tmp1"]
        msc = self.m_scr.rearrange("dc p t -> p dc t")
        for ti, (t0, tn) in enumerate(TCH):
            if ti == 4 and l == DEPTH - 1:
                continue
            who = 1 if ti == 4 else 0
            mb = ti % 2
            if mode != "first":
                P.dma("act", mprev[mb][:, :, 0:tn], msc[:, :, t0:t0 + tn], reads=[("msc", ti)], writes=[mpk[mb]],
                      allow_slow_non_contiguous=True)
            for dc in range(8):
                po, pg = (dc % 2), 2 + (dc % 2)
                for kc in range(4):
                    self.mm(self.psv(po)[:, 0:tn], wo[:, kc, dc * 128:(dc + 1) * 128], brT[:, kc, t0:t0 + tn],
                            kc == 0, kc == 3, [("two", dc), "brT"], [("ps", po)])
                for kc in range(8):
                    self.mm(self.psv(pg)[:, 0:tn], wgt[:, kc, dc * 128:(dc + 1) * 128], self.hn[:, kc, t0:t0 + tn],
                            kc == 0, kc == 7, [("twg", dc), ("hn", ti)], [("ps", pg)])
                sb_ = dc % 2
                self.act(sig[sb_][:, 0:tn], self.psv(pg)[:, 0:tn], AF.Sigmoid, [("ps", pg), "colsA"], [f"tsig{sb_}"],
                         bias=self.colsA[:, 96 + br * 8 + dc:96 + br * 8 + dc + 1])
                if mode == "first":
                    self.tt(mt[mb][:, dc, 0:tn], sig[sb_][:, 0:tn], self.psv(po)[:, 0:tn], ALU.mult,
                            [f"tsig{sb_}", ("ps", po)], [f"tm{mb}"])
                else:
                    self.tt(sig[sb_][:, 0:tn], sig[sb_][:, 0:tn], self.psv(po)[:, 0:tn], ALU.mult,
                            [f"tsig{sb_}", ("ps", po)], [f"tsig{sb_}"])
                    self.tt(mt[mb][:, dc, 0:tn], sig[sb_][:, 0:tn], mprev[mb][:, dc, 0:tn], ALU.add,
                            [f"tsig{sb_}", mpk[mb]], [f"tm{mb}"], eng="pool")
            if mode != "last":
                P.dma("sp", msc[:, :, t0:t0 + tn], mt[mb][:, :, 0:tn], reads=[f"tm{mb}"], writes=[("msc", ti)],
                      key="msc_st", allow_slow_non_contiguous=True)
                continue
            for d2 in range(8):
                pq = 4 + (d2 % 4)
                for dc in range(8):
                    self.mm(self.psv(pq)[:, 0:tn], wout[:, dc, d2 * 128:(d2 + 1) * 128], mt[mb][:, dc, 0:tn],
                            dc == 0, dc == 7, ["twout", f"tm{mb}"], [("ps", pq)])
                self.stt(self.xres[:, d2, t0:t0 + tn], self.psv(pq)[:, 0:tn], self.gsc[:, 1, d2, who:who + 1],
                         self.xres[:, d2, t0:t0 + tn], ALU.mult, ALU.add, [("ps", pq), "gsc", ("xres", ti)],
                         [("xres", ti)])
        self.P.barrier()
        self.off = m

    def fm_rms(self, src, nk, tn, nfeat, gcols, out_bf, keys_in, key_out, tag, pb):
        sqk = "t_sq"
        self.act(self.t_sq[:, 0:nk, 0:tn], src, AF.Square, keys_in, [sqk])
        for k in range(nk):
            self.mm(self.psv(pb)[:, 0:tn], self.ones_b[:], self.t_sq[:, k, 0:tn], k == 0, k == nk - 1,
                    ["ones_b", sqk], [("ps", pb)])
        self.act(self.t_rs[:, 0:tn], self.psv(pb)[:, 0:tn], AF.Sqrt, [("ps", pb), "eps_t"], ["t_rs"],
                 bias=self.eps_t[:, 0:1], scale=1.0 / nfeat)
        self.P.op("dve", lambda e: e.reciprocal(self.t_rs[:, 0:tn], self.t_rs[:, 0:tn]),
                  reads=["t_rs"], writes=["t_rs"])
        for k in range(nk):
            self.stt(out_bf[:, k, :], src[:, k, :], gcols[:, k:k + 1], self.t_rs[:, 0:tn], ALU.mult, ALU.mult,
                     keys_in + ["colsB", "t_rs"], [key_out])

    def mla(self, l):
        P, io = self.P, self.io
        m = self.off
        w_in = io["w_in"][l]
        brT = self.sb("aT", [128, 4, T], BF16)
        mm_ = self.off
        kvn = self.sb("kvn", [128, T], BF16)
        qn = self.sb("qn", [128, 2, T], BF16)
        kT = [self.sb(f"kT{i}", [96, T], BF16) for i in range(2)]
        rope = self.sb("rope", [96, 2, TL], F32)
        wuq = self.sb("m_wuq", [128, 2, 2, 768], BF16)
        wukv = self.sb("m_wukv", [128, 1024], BF16)
        rtmp = [self.sb(f"m_rt{i}", [96, 512], F32) for i in range(2)]
        m1 = self.off
        wkv = self.sb("m_wkv", [128, 8, 128], BF16)
        wq = self.sb("m_wq", [128, 8, 256], BF16)
        wkr = self.sb("m_wkr", [128, 8, 2, 96], BF16)
        self.t_sq = self.sb("t_sq", [128, 2, 512], BF16)
        self.t_rs = self.sb("t_rs", [128, 512], F32)
        raw = self.sb("m_raw", [128, 2, 512], F32)

        w3 = lambda c0, n: w_in[:, c0:c0 + n].rearrange("(kc p) c -> p kc c", p=128)
        P.dma("pool", wkv[:], w3(0, 128), writes=["m_wkv"])
        P.dma("pool", wq[:], w3(OFF_Q, 256), writes=["m_wq"])
        self.memset(wkr[:], 0.0, ["m_wkr"])
        P.dma("pool", wkr[:, :, 0, 64:96], w3(128, 32), writes=["m_wkr"], key="m_wkr")
        P.dma("pool", wkr[:, :, 1, 64:80], w3(144, 16), writes=["m_wkr"], key="m_wkr")
        P.dma("pool", wkr[:, :, 1, 80:96], w3(128, 16), writes=["m_wkr"], key="m_wkr")
        uq = io["mla_w_uq"][l].rearrange("(kc p) c -> p kc c", p=128)
        P.dma("pool", wuq[:, :, 0, :], uq, writes=["m_wuq"], key="m_wuq")
        P.dma("pool", wuq[:, :, 1, :], uq, writes=["m_wuq"], key="m_wuq")
        uq4 = io["mla_w_uq"][l].rearrange("(kc p) (h c) -> p kc h c", p=128, c=96)
        wuq5 = wuq[:].rearrange("p kc s (h c) -> p kc s h c", c=96)
        for kc in range(2):
            P.dma("pool", wuq5[:, kc, 1, :, 64:80], uq4[:, kc, :, 80:96], reads=["m_wuq"], writes=["m_wuq2"], key="m_wuq2")
            P.dma("pool", wuq5[:, kc, 1, :, 80:96], uq4[:, kc, :, 64:80], reads=["m_wuq"], writes=["m_wuq2"], key="m_wuq2")
        P.dma("pool", wukv[:], io["mla_w_ukv"][l], writes=["m_wukv"])
        P.dma("sp", rope[64:96, 0, :], io["c_rope"][0], writes=["rope"], key="rope")
        P.dma("sp", rope[64:96, 1, :], io["c_rope"][1], writes=["rope"], key="rope")

        rawq2 = self.sb("m_rawq2", [128, 2, 512], F32)
        rawq = [raw, rawq2]
        rawk1 = self.sb("m_rawk1", [128, 1, 512], F32)
        rawk = [rawk1, rawk1]

        def m1_proj(ti):
            t0, tn = TCH[ti]
            hk = ("hn", ti)
            for kc in range(8):
                self.mm(self.psv(0)[:, 0:tn], wkv[:, kc, :], self.hn[:, kc, t0:t0 + tn], kc == 0, kc == 7,
                        ["m_wkv", hk], [("ps", 0)])
            for qc in range(2):
                for kc in range(8):
                    self.mm(self.psv(1 + qc)[:, 0:tn], wq[:, kc, qc * 128:(qc + 1) * 128], self.hn[:, kc, t0:t0 + tn],
                            kc == 0, kc == 7, ["m_wq", hk], [("ps", 1 + qc)])
            for s_ in range(2):
                for kc in range(8):
                    self.mm(self.psv(3 + s_)[0:96, 0:tn], wkr[:, kc, s_, :], self.hn[:, kc, t0:t0 + tn], kc == 0, kc == 7,
                            ["m_wkr", hk], [("ps", 3 + s_)])

        def m1_evac(ti):
            t0, tn = TCH[ti]
            b = ti % 2
            rk, rq = "m_rawk", f"m_rawq{b}"
            self.cp(rawk[b][:, 0, 0:tn], self.psv(0)[:, 0:tn], [("ps", 0)], [rk], eng="act")
            for qc in range(2):
                self.cp(rawq[b][:, qc, 0:tn], self.psv(1 + qc)[:, 0:tn], [("ps", 1 + qc)], [rq], eng="act")
            if ti < 4:
                self.tt(rtmp[0][64:96, 0:tn], self.psv(3)[64:96, 0:tn], rope[64:96, 0, t0:t0 + tn], ALU.mult,
                        [("ps", 3), "rope"], ["m_rt0"])
                self.tt(rtmp[1][64:96, 0:tn], self.psv(4)[64:96, 0:tn], rope[64:96, 1, t0:t0 + tn], ALU.mult,
                        [("ps", 4), "rope"], ["m_rt1"])
                self.tt(kT[0][64:96, t0:t0 + tn], rtmp[0][64:96, 0:tn], rtmp[1][64:96, 0:tn], ALU.add,
                        ["m_rt0", "m_rt1"], ["kT0"])
            else:
                self.cp(kT[0][64:96, t0:t0 + tn], self.psv(3)[64:96, 0:tn], [("ps", 3), ("ps", 4)], ["kT0"], eng="dve")
            self.cp(kT[1][64:96, t0:t0 + tn], kT[0][64:96, t0:t0 + tn], ["kT0"], ["kT1"], eng="pool")

        def m1_chain(ti):
            t0, tn = TCH[ti]
            b = ti % 2
            rk, rq = "m_rawk", f"m_rawq{b}"
            self.fm_rms(rawk[b][:, 0:1, 0:tn], 1, tn, 128, self.colsB[:, 10:11], kvn[:, t0:t0 + tn].unsqueeze(1),
                        [rk], "kvn", "mkv", 7)
            self.fm_rms(rawq[b][:, 0:2, 0:tn], 2, tn, 256, self.colsB[:, 8:10], qn[:, :, t0:t0 + tn],
                        [rq], "qn", "mq", 6)

        m1_proj(0)
        for ti in range(5):
            m1_evac(ti)
            if ti + 1 < 5:
                m1_proj(ti + 1)
            m1_chain(ti)
        self.dump("kvn", kvn[:], ["kvn"])
        self.dump("qn", qn[:], ["qn"])
        self.dump("krr", kT[0][64:96, :], ["kT0"])

        self.P.barrier()
        self.off = m1
        qT = [self.sb(f"qT{i}", [96, T], BF16) for i in range(2)]
        va = [self.sb(f"va{i}", [128, 18, 128], BF16) for i in range(2)]
        ex = [self.sb(f"ex{i}", [128, 512], BF16) for i in range(3)]
        rden = rtmp[1][0:64, :]
        for i in range(2):
            self.memset(va[i][:, :, 64:128], 1.0, [f"va{i}"])

        def expand_items(h):
            hb = h % 2
            kTk, qTk, vak = f"kT{hb}", f"qT{hb}", f"va{hb}"
            items = []
            for ti, (t0, tn) in enumerate(TCH):
                def f(ti=ti, t0=t0, tn=tn):
                    self.mm(self.psv(5)[0:64, 0:tn], wukv[:, h * 128:h * 128 + 64], kvn[:, t0:t0 + tn], True, True,
                            ["m_wukv", "kvn"], [("ps", 5)])
                    self.cp(kT[hb][0:64, t0:t0 + tn], self.psv(5)[0:64, 0:tn], [("ps", 5)], [kTk], eng="dve")
                    for s_ in range(2):
                        for kc in range(2):
                            self.mm(self.psv(6 + s_)[0:96, 0:tn], wuq[:, kc, s_, h * 96:(h + 1) * 96], qn[:, kc, t0:t0 + tn],
                                    kc == 0, kc == 1, ["m_wuq", "m_wuq2", "qn"], [("ps", 6 + s_)])
                    self.cp(qT[hb][0:64, t0:t0 + tn], self.psv(6)[0:64, 0:tn], [("ps", 6)], [qTk], eng="dve")
                    if ti < 4:
                        self.tt(rtmp[0][64:96, 0:tn], self.psv(6)[64:96, 0:tn], rope[64:96, 0, t0:t0 + tn], ALU.mult,
                                [("ps", 6), "rope"], ["m_rt0"])
                        self.tt(rtmp[1][64:96, 0:tn], self.psv(7)[64:96, 0:tn], rope[64:96, 1, t0:t0 + tn], ALU.mult,
                                [("ps", 7), "rope"], ["m_rt1"])
                        self.tt(qT[hb][64:96, t0:t0 + tn], rtmp[0][64:96, 0:tn], rtmp[1][64:96, 0:tn], ALU.add,
                                ["m_rt0", "m_rt1"], [qTk])
                    else:
                        self.cp(qT[hb][64:96, t0:t0 + tn], self.psv(6)[64:96, 0:tn], [("ps", 6), ("ps", 7)], [qTk], eng="dve")
                items.append(f)

            def fv():
                for kt in range(18):
                    bank = 5 + kt // 8
                    self.mm(self.psv(bank)[:, (kt % 8) * 64:(kt % 8) * 64 + 64], kvn[:, kt * 128:(kt + 1) * 128],
                            wukv[:, h * 128 + 64:h * 128 + 128], True, True, ["kvn", "m_wukv"], [("ps", bank)],
                            signal=(kt % 8 == 7 or kt == 17))
                for bank in range(5, 8):
                    n = 8 if bank < 7 else 2
                    self.cp(va[hb][:, (bank - 5) * 8:(bank - 5) * 8 + n, 0:64],
                            self.psv(bank)[:, 0:n * 64].rearrange("p (a b) -> p a b", b=64), [("ps", bank)], [vak], eng="dve")
            items.append(fv)
            return items

        for f in expand_items(0):
            f()
        self.dump("kT0", kT[0][:], ["kT0"])
        self.dump("qT0", qT[0][:], ["qT0"])
        self.dump("va0", va[0][:], ["va0"])
        LA = 2
        gi = 0
        for h in range(8):
            hb = h % 2
            kTk, qTk, vak = f"kT{hb}", f"qT{hb}", f"va{hb}"
            pend = expand_items(h + 1) if h + 1 < 8 else []
            its = []
            for qi, (q0, qn_) in enumerate(TCH):
                if qi == 4 and l == DEPTH - 1:
                    continue
                kts = list(range(18)) if qi < 4 else [16, 17]
                for n_, kt in enumerate(kts):
                    its.append((qi, q0, qn_, n_, kt, len(kts)))
            n_it = len(its)
            every = max(1, n_it // (len(pend) + 1)) if pend else 0

            def S(i):
                qi, q0, qn_, n_, kt, nk = its[i]
                b3 = (gi + i) % 3
                self.mm(self.psv(b3)[:, 0:qn_], kT[hb][:, kt * 128:(kt + 1) * 128], qT[hb][:, q0:q0 + qn_],
                        True, True, [kTk, qTk], [("ps", b3)])
                self.act(ex[b3][:, 0:qn_], self.psv(b3)[:, 0:qn_], AF.Exp, [("ps", b3)], [f"ex{b3}"], scale=ATTN_SCALE)

            def PV(i):
                qi, q0, qn_, n_, kt, nk = its[i]
                b3 = (gi + i) % 3
                po = 3 + (qi % 2)
                self.mm(self.psv(po)[:, 0:qn_], va[hb][:, kt, :], ex[b3][:, 0:qn_], n_ == 0, n_ == nk - 1,
                        [vak, f"ex{b3}"], [("ps", po)])
                if n_ == nk - 1:
                    self.P.op("dve", lambda e: e.reciprocal(rden[:, 0:qn_], self.psv(po)[64:128, 0:qn_]),
                              reads=[("ps", po)], writes=["m_rt1"])
                    self.tt(brT[(h % 2) * 64:(h % 2) * 64 + 64, h // 2, q0:q0 + qn_], self.psv(po)[0:64, 0:qn_],
                            rden[:, 0:qn_], ALU.mult, [("ps", po), "m_rt1"], ["brT"])

            for i in range(n_it + LA):
                if i < n_it:
                    S(i)
                if i >= LA:
                    PV(i - LA)
                if pend and i % every == every - 1:
                    pend.pop(0)()
            while pend:
                pend.pop(0)()
            gi += n_it
        self.dump("aT", brT[:], ["brT"])
        self.P.barrier()
        self.off = mm_
        self.tail(l, 0, brT, "mla_w_o", "first")
        self.off = m

    def gmlp(self, l):
        P, io = self.P, self.io
        m = self.off
        w_in = io["w_in"][l]
        brT = self.sb("gT", [128, 4, T], BF16)
        mm_ = self.off
        self.sel = self.sb("sel", [2, 128], F32)
        P.dma("sp", self.sel[:], io["c_sel"], writes=["sel"])
        wuv = self.sb("g_wuv", [128, 8, 1024], BF16)
        wsT = self.sb("g_wsT", [128, 8, 128], BF16)
        wsr = self.sb("g_wsr", [128, 128], F32)
        bsr = self.sb("g_bsr", [2, 4, 128], F32)
        bsb = self.sb("g_bsb", [128, 4, 128], F32)
        uT = self.sb("g_uT", [128, 4, 512], BF16)
        v32 = self.sb("g_v32", [128, 4, 512], F32)
        vb = self.sb("g_vb", [128, 4, 512], BF16)
        vsq = self.sb("g_vsq", [128, 4, 512], BF16)
        mean = self.sb("g_mean", [128, 512], F32)
        rstd = self.sb("g_rstd", [128, 512], F32)
        vc = self.sb("g_vc", [128, 512], F32)
        vn = self.sb("g_vn", [128, 4, 512], BF16)
        vtm = [self.sb(f"g_vtm{i}", [128, 512], BF16) for i in range(2)]
        mx = [self.sb(f"g_mx{i}", [128, 4, 128], F32) for i in range(2)]
        P.dma("pool", wuv[:], w_in[:, OFF_UV:OFF_UV + 1024].rearrange("(kc p) c -> p kc c", p=128), writes=["g_wuv"])
        for g in range(8):
            P.dma("sp", wsr[:], io["gm_w_s"][l, g], writes=["g_wsr"])
            self.tr(self.psv(0)[:, 0:128], wsr[:], self.ident[:], ["g_wsr", "ident"], [("ps", 0)])
            self.cp(wsT[:, g, :], self.psv(0)[:, 0:128], [("ps", 0)], ["g_wsT"], eng="dve")
        P.dma("sp", bsr[:], io["gm_b_s"][l].rearrange("(pair k) i -> k pair i", k=2), writes=["g_bsr"])
        self.mm(self.psv(1)[:, 0:512], self.sel[:], bsr[:].rearrange("k a i -> k (a i)"), True, True,
                ["sel", "g_bsr"], [("ps", 1)])
        self.cp(bsb[:], self.psv(1)[:, 0:512].rearrange("p (a i) -> p a i", i=128), [("ps", 1)], ["g_bsb"], eng="dve")
        inv = 1.0 / 512
        uT2 = [uT, self.sb("g_uT2", [128, 4, 512], BF16)]
        v322 = [v32, self.sb("g_v322", [128, 4, 512], F32)]
        tis = [ti for ti in range(5) if not (ti == 4 and l == DEPTH - 1)]

        def g_proj(ti):
            t0, tn = TCH[ti]
            b = ti % 2
            hk = ("hn", ti)
            for oc in range(8):
                pb = oc % 2
                for kc in range(8):
                    self.mm(self.psv(pb)[:, 0:tn], wuv[:, kc, oc * 128:(oc + 1) * 128], self.hn[:, kc, t0:t0 + tn],
                            kc == 0, kc == 7, ["g_wuv", hk], [("ps", pb)])
                if oc < 4:
                    self.act(uT2[b][:, oc, 0:tn], self.psv(pb)[:, 0:tn], AF.Gelu, [("ps", pb)], [f"g_uT{b}"])
                else:
                    self.act(v322[b][:, oc - 4, 0:tn], self.psv(pb)[:, 0:tn], AF.Gelu, [("ps", pb)], [f"g_v32{b}"])

        def g_mix(ti):
            t0, tn = TCH[ti]
            b = ti % 2
            uT_, v32_ = uT2[b], v322[b]
            uk, vk = f"g_uT{b}", f"g_v32{b}"
            self.cp(vb[:, :, 0:tn], v32_[:, :, 0:tn], [vk], ["g_vb"], eng="pool")
            self.act(vsq[:, :, 0:tn], v32_[:, :, 0:tn], AF.Square, [vk], ["g_vsq"])
            for k in range(4):
                self.mm(self.psv(2)[:, 0:tn], self.ones_b[:], vb[:, k, 0:tn], k == 0, k == 3, ["ones_b", "g_vb"], [("ps", 2)])
            for k in range(4):
                self.mm(self.psv(3)[:, 0:tn], self.ones_b[:], vsq[:, k, 0:tn], k == 0, k == 3, ["ones_b", "g_vsq"], [("ps", 3)])
            self.ts(mean[:, 0:tn], self.psv(2)[:, 0:tn], inv, None, ALU.mult, None, [("ps", 2)], ["g_mean"])
            self.tt(vc[:, 0:tn], mean[:, 0:tn], mean[:, 0:tn], ALU.mult, ["g_mean"], ["g_vc"])
            self.stt(rstd[:, 0:tn], self.psv(3)[:, 0:tn], inv, vc[:, 0:tn], ALU.mult, ALU.subtract,
                     [("ps", 3), "g_vc"], ["g_rstd"])
            self.act(rstd[:, 0:tn], rstd[:, 0:tn], AF.Sqrt, ["g_rstd", "eps_t"], ["g_rstd"], bias=self.eps_t[:, 0:1])
            self.P.op("dve", lambda e: e.reciprocal(rstd[:, 0:tn], rstd[:, 0:tn]), reads=["g_rstd"], writes=["g_rstd"])
            for k in range(4):
                self.tt(vc[:, 0:tn], v32_[:, k, 0:tn], mean[:, 0:tn], ALU.subtract, [vk, "g_mean"], ["g_vc"])
                self.stt(vn[:, k, 0:tn], vc[:, 0:tn], self.colsB[:, 4 + k:5 + k], rstd[:, 0:tn], ALU.mult, ALU.mult,
                         ["g_vc", "colsB", "g_rstd"], ["g_vn"])
            for c in range(tn // 128):
                cb = c % 2
                pbt = self.psv(4, BF16)
                for k in range(4):
                    self.tr(pbt[:, k * 128:(k + 1) * 128], vn[:, k, c * 128:(c + 1) * 128], self.identb[:],
                            ["g_vn", "identb"], [("ps", 4)], signal=(k == 3))
                self.cp(vtm[cb][:], pbt[:, 0:512], [("ps", 4)], [f"g_vtm{cb}"], eng="act")
                for g in range(8):
                    pair = g // 2
                    pm = 5 + (g % 2)
                    self.mm(self.psv(pm)[:, pair * 128:(pair + 1) * 128], vtm[cb][:, pair * 128:(pair + 1) * 128],
                            wsT[:, g, :], True, True, [f"g_vtm{cb}", "g_wsT"], [("ps", pm)], signal=(g >= 6))
                self.tt(mx[cb][0:64], self.psv(5)[0:64, :].rearrange("p (a i) -> p a i", i=128), bsb[0:64], ALU.add,
                        [("ps", 5), "g_bsb"], [f"g_mx{cb}"])
                self.tt(mx[cb][64:128], self.psv(6)[64:128, :].rearrange("p (a i) -> p a i", i=128), bsb[64:128], ALU.add,
                        [("ps", 6), "g_bsb"], [f"g_mx{cb}"])
                self.tt(brT[:, :, t0 + c * 128:t0 + (c + 1) * 128], mx[cb][:], uT_[:, :, c * 128:(c + 1) * 128], ALU.mult,
                        [f"g_mx{cb}", uk], ["brT"], eng="pool")

        g_proj(tis[0])
        for n_, ti in enumerate(tis):
            if n_ + 1 < len(tis):
                g_proj(tis[n_ + 1])
            g_mix(ti)
        self.dump("gT", brT[:], ["brT"])
        self.P.barrier()
        self.off = mm_
        self.tail(l, 2, brT, "gm_w_o", "mid")
        self.off = m

    def ssd(self, l):
        P, io = self.P, self.io
        m = self.off
        w_in = io["w_in"][l]
        brT = self.sb("sT", [128, 4, T], BF16)
        mm_ = self.off
        dt_all = self.sb("dt_all", [128, 18, 16], F32)
        a_all = self.sb("a_all", [128, 18, 16], F32)
        ma = self.off
        wx = self.sb("s_wx", [128, 8, 1024], BF16)
        wdt = self.sb("s_wdt", [128, 8, 16], BF16)
        rawt = [self.sb(f"s_raw{i}", [128, 2312], F32) for i in range(2)]
        acc = [self.sb(f"s_acc{i}", [128, 2308], F32) for i in range(2)]
        xst = [self.sb(f"s_xst{i}", [128, T], BF16) for i in range(2)]
        P.dma("pool", wx[:], w_in[:, 160:1184].rearrange("(kc p) c -> p kc c", p=128), writes=["s_wx"])
        P.dma("pool", wdt[:], w_in[:, 1184:1200].rearrange("(kc p) c -> p kc c", p=128), writes=["s_wdt"])
        for i in range(2):
            self.memset(rawt[i][:], 0.0, [f"s_raw{i}"])
        for cc in range(8):
            b = cc % 2
            rk, ak, xk = f"s_raw{b}", f"s_acc{b}", f"s_xst{b}"
            for ti, (t0, tn) in enumerate(TCH):
                pb = ti % 4
                for kc in range(8):
                    self.mm(self.psv(pb)[:, 0:tn], wx[:, kc, cc * 128:(cc + 1) * 128], self.hn[:, kc, t0:t0 + tn],
                            kc == 0, kc == 7, ["s_wx", ("hn", ti)], [("ps", pb)])
                c0 = 2 + t0 if ti < 4 else 2054
                self.cp(rawt[b][:, c0:c0 + tn], self.psv(pb)[:, 0:tn], [("ps", pb)], [rk], eng="act")
            self.ts(acc[b][:], rawt[b][:, 0:2308], self.convw[:, cc, 0:1], self.colsA[:, 120 + cc:121 + cc], ALU.mult, ALU.add,
                    [rk, "convw", "colsA"], [ak])
            for k in range(1, 5):
                self.stt(acc[b][:], rawt[b][:, k:k + 2308], self.convw[:, cc, k:k + 1], acc[b][:], ALU.mult, ALU.add,
                         [rk, "convw", ak], [ak])
            self.act(xst[b][:, 0:TL], acc[b][:, 0:TL], AF.Silu, [ak], [xk])
            self.act(xst[b][:, TL:T], acc[b][:, 2052:2308], AF.Silu, [ak], [xk])
            P.dma("sp", self.xbc_scr[cc], xst[b][:], reads=[xk], writes=["xbc_scr"], key="xbc_st")
        for tt_ in range(18):
            ti = min(tt_ // 4, 4)
            pb = 4 + tt_ % 2
            for kc in range(8):
                self.mm(self.psv(pb)[:, 0:16], self.hn[:, kc, tt_ * 128:(tt_ + 1) * 128], wdt[:, kc, :], kc == 0, kc == 7,
                        [("hn", ti), "s_wdt"], [("ps", pb)])
            self.tt(dt_all[:, tt_, :], self.psv(pb)[:, 0:16], self.ssm_bc[:, 16:32], ALU.add, [("ps", pb), "ssm_bc"], ["dt_all"])
        self.act(dt_all[:], dt_all[:], AF.Exp, ["dt_all"], ["dt_all"])
        self.act(dt_all[:], dt_all[:], AF.Ln, ["dt_all"], ["dt_all"], bias=self.ones_f[:, 0:1])
        self.stt(a_all[:], dt_all[:], -1.0, self.expA[:].unsqueeze(1).to_broadcast([128, 18, 16]), ALU.mult, ALU.mult,
                 ["dt_all", "expA"], ["a_all"])
        self.dump("dt_all", dt_all[:], ["dt_all"])
        self.P.barrier()
        self.off = ma
        wz = self.sb("s_wz", [128, 8, 512], BF16)
        P.dma("pool", wz[:], w_in[:, OFF_Z:OFF_Z + 512].rearrange("(kc p) c -> p kc c", p=128), writes=["s_wz"])
        st = self.sb("s_st", [128, 512], F32)
        hpb = self.sb("s_hpb", [128, 512], BF16)
        xcb = [self.sb(f"s_xc{i}", [128, 8, 128], BF16) for i in range(3)]
        NB = 2

        def mk(name, shape, dt):
            return [self.sb(f"{name}{i}", shape, dt) for i in range(NB)]
        xtm = mk("s_xtm", [128, 512], BF16)
        xdt = mk("s_xdt", [128, 512], BF16)
        xw = mk("s_xw", [128, 512], BF16)
        btm = mk("s_btm", [128, 256], BF16)
        E = mk("s_E", [128, 24], F32)
        rhsm = mk("s_rhsm", [128, 8, 128], F32)
        dec = mk("s_dec", [128, 8, 128], BF16)
        cbm = mk("s_cbm", [128, 2, 128], BF16)
        scT = mk("s_scT", [128, 8, 128], BF16)
        y1 = mk("s_y1", [128, 512], F32)
        y2 = mk("s_y2", [128, 512], F32)
        yfb = mk("s_yfb", [128, 512], BF16)
        zs = mk("s_zs", [128, 512], F32)
        ssq = mk("s_ssq", [128, 1], F32)
        ynb = mk("s_ynb", [128, 512], BF16)
        dbc = self.ssm_bc[:, 32:40]
        order_f = [16, 17] + list(range(16))
        order_b = [17, 16] + list(range(15, -1, -1))
        seq = [(0, ci) for ci in order_f] + [(1, ci) for ci in order_b]
        bc = lambda ap: ap.unsqueeze(2).to_broadcast([128, 8, 64])
        v3 = lambda t_: t_[:].rearrange("p (h q) -> p h q", q=64)
        xbc_v = self.xbc_scr.rearrange("cc p t -> p cc t")

        def load(n):
            d_, ci = seq[n]
            tok = ci * 128
            xb = n % 3
            P.dma("sp", xcb[xb][:], xbc_v[:, :, tok:tok + 128], reads=["xbc_scr"], writes=[f"s_xc{xb}"],
                  allow_slow_non_contiguous=True)

        def stageA(n):
            d_, ci = seq[n]
            s_ = n % NB
            xb = n % 3
            xk = f"s_xc{xb}"
            K = lambda nm: f"{nm}{s_}"
            tok = ci * 128
            ti = min(ci // 4, 4)
            mLE, mGT = (0, 1) if d_ == 0 else (2, 3)
            a_c = a_all[:, ci, d_ * 8:(d_ + 1) * 8]
            dt_c = dt_all[:, ci, d_ * 8:(d_ + 1) * 8]
            xc = xcb[xb]
            if d_ == 1:
                P.dma("act", yfb[s_][:], self.yf_scr[ci], reads=[("yf", ci)], writes=[K("s_yfb")], key=f"yf_ld{s_}")
            self.tt(rhsm[s_][:], a_c.unsqueeze(2).to_broadcast([128, 8, 128]),
                    self.masks[:, mLE, :].unsqueeze(1).to_broadcast([128, 8, 128]), ALU.mult,
                    ["a_all", "masks"], [K("s_rhsm")], eng="pool")
            p0 = self.psv(0, BF16)
            for k in range(4):
                self.tr(p0[:, k * 128:(k + 1) * 128], xc[:, k, :], self.identb[:], [xk, "identb"], [("ps", 0)], signal=False)
            for k in range(2):
                self.tr(p0[:, 512 + k * 128:512 + (k + 1) * 128], xc[:, 4 + k, :], self.identb[:], [xk, "identb"],
                        [("ps", 0)], signal=(k == 1))
            self.cp(xtm[s_][:], p0[:, 0:512], [("ps", 0)], [K("s_xtm")], eng="act")
            self.cp(btm[s_][:], p0[:, 512:768], [("ps", 0)], [K("s_btm")], eng="act")
            self.mm(self.psv(1)[:, 0:8], self.masks[:, mLE, :], a_c, True, True, ["masks", "a_all"], [("ps", 1)], signal=False)
            self.mm(self.psv(1)[:, 8:16], self.masks[:, mGT, :], a_c, True, True, ["masks", "a_all"], [("ps", 1)], signal=False)
            self.mm(self.psv(1)[:, 16:24], self.ones_f[:], a_c, True, True, ["ones_f", "a_all"], [("ps", 1)])
            self.act(E[s_][:], self.psv(1)[:, 0:24], AF.Exp, [("ps", 1)], [K("s_E")])
            self.tt(v3(xdt[s_]), v3(xtm[s_]), bc(dt_c), ALU.mult, [K("s_xtm"), "dt_all"], [K("s_xdt")], eng="pool")
            self.tt(v3(xw[s_]), v3(xdt[s_]), bc(E[s_][:, 8:16]), ALU.mult, [K("s_xdt"), K("s_E")], [K("s_xw")], eng="pool")
            if d_ == 1:
                self.tt(v3(y2[s_]), v3(xtm[s_]), bc(dbc), ALU.mult, [K("s_xtm"), "ssm_bc"], [K("s_y2")], eng="pool")
            for g in range(2):
                self.mm(self.psv(6)[:, g * 128:(g + 1) * 128], xc[:, 4 + g, :], xc[:, 6 + g, :],
                        True, True, [xk], [("ps", 6)], signal=(g == 1))
            self.tt(cbm[s_][:], self.psv(6)[:, 0:256].rearrange("p (g i) -> p g i", i=128),
                    self.masks_b[:, d_, :].unsqueeze(1).to_broadcast([128, 2, 128]), ALU.mult,
                    [("ps", 6), "masks_b"], [K("s_cbm")])
            for hf in range(2):
                self.mm(self.psv(4), self.masks[:, mGT, :], rhsm[s_][:, hf * 4:(hf + 1) * 4, :].rearrange("p a b -> p (a b)"),
                        True, True, ["masks", K("s_rhsm")], [("ps", 4)])
                self.act(dec[s_][:, hf * 4:(hf + 1) * 4, :].rearrange("p a b -> p (a b)"), self.psv(4), AF.Exp,
                         [("ps", 4)], [K("s_dec")])
                self.tt(scT[s_][:, hf * 4:(hf + 1) * 4, :], dec[s_][:, hf * 4:(hf + 1) * 4, :],
                        cbm[s_][:, hf, :].unsqueeze(1).to_broadcast([128, 4, 128]), ALU.mult,
                        [K("s_dec"), K("s_cbm")], [K("s_scT")])
            if d_ == 1:
                for kc in range(8):
                    self.mm(self.psv(5), self.hn[:, kc, tok:tok + 128], wz[:, kc, :], kc == 0, kc == 7,
                            [("hn", ti), "s_wz"], [("ps", 5)])
                self.act(zs[s_][:], self.psv(5), AF.Silu, [("ps", 5)], [K("s_zs")])

        def stageB(n):
            d_, ci = seq[n]
            s_ = n % NB
            xb = n % 3
            xk = f"s_xc{xb}"
            K = lambda nm: f"{nm}{s_}"
            xc = xcb[xb]
            if ci == 16 + d_ and n in (0, 18):
                self.memset(st[:], 0.0, ["s_st"])
                self.memset(hpb[:], 0.0, ["s_hpb"])
            for g in range(2):
                self.mm(self.psv(2)[:, g * 256:(g + 1) * 256], btm[s_][:, g * 128:(g + 1) * 128],
                        xw[s_][:, g * 256:(g + 1) * 256], True, True, [K("s_btm"), K("s_xw")], [("ps", 2)], signal=(g == 1))
            for g in range(2):
                self.mm(self.psv(3)[:, g * 256:(g + 1) * 256], xc[:, 6 + g, :], hpb[:, g * 256:(g + 1) * 256],
                        True, True, [xk, "s_hpb"], [("ps", 3)], signal=(g == 1))
            for h in range(8):
                self.mm(self.psv(7)[:, h * 64:(h + 1) * 64], scT[s_][:, h, :], xdt[s_][:, h * 64:(h + 1) * 64], True, True,
                        [K("s_scT"), K("s_xdt")], [("ps", 7)], signal=(h == 7))
            self.tt(v3(st), v3(st), bc(E[s_][:, 16:24]), ALU.mult, ["s_st", K("s_E")], ["s_st"])
            self.tt(st[:], st[:], self.psv(2), ALU.add, ["s_st", ("ps", 2)], ["s_st"])
            self.cp(hpb[:], st[:], ["s_st"], ["s_hpb"], eng="act")
            self.tt(v3(y1[s_]), self.psv(3).rearrange("p (h q) -> p h q", q=64), bc(E[s_][:, 0:8]), ALU.mult,
                    [("ps", 3), K("s_E")], [K("s_y1")])
            self.tt(y1[s_][:], y1[s_][:], self.psv(7), ALU.add, [K("s_y1"), ("ps", 7)], [K("s_y1")])
            if d_ == 0:
                self.cp(yfb[s_][:], y1[s_][:], [K("s_y1")], [K("s_yfb")], eng="act")
                P.dma("sp", self.yf_scr[ci], yfb[s_][:], reads=[K("s_yfb")], writes=[("yf", ci)], key="yf_st")
            else:
                self.tt(y1[s_][:], y1[s_][:], yfb[s_][:], ALU.add, [K("s_y1"), K("s_yfb")], [K("s_y1")])
                self.tt(y1[s_][:], y1[s_][:], y2[s_][:], ALU.add, [K("s_y1"), K("s_y2")], [K("s_y1")])
                self.tt(y1[s_][:], y1[s_][:], zs[s_][:], ALU.mult, [K("s_y1"), K("s_zs")], [K("s_y1")])
                self.P.op("act", lambda e: e.activation(y2[s_][:], y1[s_][:], AF.Square, accum_out=ssq[s_][:]),
                          reads=[K("s_y1")], writes=[K("s_y2"), K("s_ssq")])
                self.act(ssq[s_][:], ssq[s_][:], AF.Sqrt, [K("s_ssq"), "eps_t"], [K("s_ssq")], bias=self.eps_t[:, 0:1],
                         scale=1.0 / 512)
                self.P.op("dve", lambda e: e.reciprocal(ssq[s_][:], ssq[s_][:]), reads=[K("s_ssq")], writes=[K("s_ssq")])
                self.ts(ynb[s_][:], y1[s_][:], ssq[s_][:, 0:1], None, ALU.mult, None, [K("s_y1"), K("s_ssq")], [K("s_ynb")])

        def stageC(n):
            d_, ci = seq[n]
            if d_ == 0:
                return
            s_ = n % NB
            K = lambda nm: f"{nm}{s_}"
            tok = ci * 128
            p7 = self.psv(7, BF16)
            for k in range(4):
                self.tr(p7[:, k * 128:(k + 1) * 128], ynb[s_][:, k * 128:(k + 1) * 128], self.identb[:],
                        [K("s_ynb"), "identb"], [("ps", 7)], signal=(k == 3))
            for k in range(4):
                self.ts(brT[:, k, tok:tok + 128], p7[:, k * 128:(k + 1) * 128], self.colsB[:, k:k + 1], None,
                        ALU.mult, None, [("ps", 7), "colsB"], ["brT"])

        load(0)
        load(1)
        N = len(seq)
        for n in range(N + 2):
            if n < N:
                stageA(n)
            if 1 <= n <= N:
                stageB(n - 1)
            if n + 2 < N:
                load(n + 2)
            if n >= 2:
                stageC(n - 2)
        self.dump("sT", brT[:], ["brT"])
        self.P.barrier()
        self.off = mm_
        self.tail(l, 1, brT, "ssm_w_o", "last")
        self.off = m

    def layer(self, l):
        self.set_layer(l)
        self.dump(f"mod{l}", self.mod[:], ["mod"])
        if self.stop_after == f"L{l}params":
            return
        self.ffn(l, 1)
        self.dump(f"xres_ffn1_{l}", self.xres[:], [("xres", i) for i in range(5)])
        if self.stop_after == f"L{l}ffn1":
            return
        self.prenorm(1)
        self.dump(f"hn_mix_{l}", self.hn[:], [("hn", i) for i in range(5)])
        if self.stop_after == f"L{l}hn":
            return
        self.mla(l)
        self.dump(f"xres_mla_{l}", self.xres[:], [("xres", i) for i in range(5)])
        if self.stop_after == f"L{l}mla":
            return
        self.gmlp(l)
        self.dump(f"xres_gm_{l}", self.xres[:], [("xres", i) for i in range(5)])
        if self.stop_after == f"L{l}gm":
            return
        self.ssd(l)
        self.dump(f"xres_ssd_{l}", self.xres[:], [("xres", i) for i in range(5)])
        if self.stop_after == f"L{l}ssd":
            return
        self.ffn(l, 2, extra=(self.params_steps(l + 1) if l + 1 < DEPTH else None))
        self.dump(f"xres_ffn2_{l}", self.xres[:], [("xres", i) for i in range(5)])

    def final(self):
        P = self.P
        m = self.off
        sq = [self.sb(f"fsq{i}", [128, 8, 512], BF16) for i in range(2)]
        rs = [self.sb(f"frs{i}", [128, 512], F32) for i in range(2)]
        xn = [self.sb(f"fxn{i}", [128, 8, 512], F32) for i in range(2)]
        ot = [self.sb(f"fot{i}", [128, D], F32) for i in range(2)]
        oi = 0
        for ti in range(4):
            t0, tn = TCH[ti]
            b = ti % 2
            xk = ("xres", ti)
            self.act(sq[b][:], self.xres[:, :, t0:t0 + tn], AF.Square, [xk], [f"fsq{b}"])
            for dc in range(8):
                self.mm(self.psv(b), self.ones_b[:], sq[b][:, dc, :], dc == 0, dc == 7, ["ones_b", f"fsq{b}"], [("ps", b)])
            self.act(rs[b][:], self.psv(b), AF.Sqrt, [("ps", b), "eps_t"], [f"frs{b}"], bias=self.eps_t[:, 0:1], scale=1.0 / D)
            self.P.op("dve", lambda e, b=b: e.reciprocal(rs[b][:], rs[b][:]), reads=[f"frs{b}"], writes=[f"frs{b}"])
            for dc in range(8):
                self.stt(xn[b][:, dc, :], self.xres[:, dc, t0:t0 + tn], self.colsB[:, 11 + dc:12 + dc], rs[b][:],
                         ALU.mult, ALU.mult, [xk, "colsB", f"frs{b}"], [f"fxn{b}"])
            for c in range(4):
                ob = oi % 2
                oi += 1
                for half in range(2):
                    pb = 2 + (oi * 2 + half) % 4
                    for k in range(4):
                        dc = half * 4 + k
                        self.tr(self.psv(pb)[:, k * 128:(k + 1) * 128], xn[b][:, dc, c * 128:(c + 1) * 128], self.ident[:],
                                [f"fxn{b}", "ident"], [("ps", pb)], signal=(k == 3))
                    self.cp(ot[ob][:, half * 512:(half + 1) * 512], self.psv(pb), [("ps", pb)], [f"fot{ob}"])
                r0 = t0 + c * 128
                P.dma("sp", self.out[r0:r0 + 128, :], ot[ob][:], reads=[f"fot{ob}"], key="out")
        P.wait_all_dma("sp", ["out"])
        self.off = m


def build_program(dbg_specs=None, stop_after=None):
    nc = bass.Bass("TRN2", target_bir_lowering=False)
    stack = ExitStack()
    with stack:
        P = Prog(nc, stack)
        dbg = None
        if dbg_specs:
            dbg = {}
            for name, (shape, dt) in dbg_specs.items():
                dbg[name] = nc.dram_tensor("dbg_" + name, shape, dt, kind="ExternalOutput").ap()
        b = Builder(nc, P, dbg=dbg, stop_after=stop_after)
        b.build()
        P.emit()
    return nc


def _consts():
    ident = np.eye(128, dtype=np.float32)
    s = np.arange(128)[:, None]
    t = np.arange(128)[None, :]
    masks = np.stack([(s <= t), (s > t), (s >= t), (s < t)]).astype(np.float32)
    rows = TL // 64
    row = np.repeat(np.arange(rows, dtype=np.float32), 64)
    col = np.tile(np.arange(64, dtype=np.float32), rows)
    inv = np.power(np.float32(10000.0), -np.arange(8, dtype=np.float32) / np.float32(8)).astype(np.float32)
    ang = np.concatenate([row[:, None] * inv, col[:, None] * inv], axis=-1).astype(np.float32)
    cos = np.cos(ang).astype(np.float32).T
    sin = np.sin(ang).astype(np.float32).T
    cos2 = np.concatenate([cos, cos], axis=0)
    sin2s = np.concatenate([-sin, sin], axis=0)
    rope = np.stack([cos2, sin2s]).astype(np.float32)
    sel = np.zeros((2, 128), np.float32)
    sel[0, :64] = 1.0
    sel[1, 64:] = 1.0
    return ident, masks, rope, sel


def make_in_maps(inp, n_cores=8):
    f = lambda a: np.ascontiguousarray(np.asarray(a, dtype=np.float32))
    ident, masks, rope, sel = _consts()
    shared = {
        "w_ada": f(inp["w_ada"]), "b_ada": f(inp["b_ada"]).reshape(DEPTH, 72, 128),
        "norm_g": f(inp["norm_g"]).reshape(DEPTH, 24, 128),
        "ffn1_w_in": f(inp["ffn1_w_in"]), "ffn1_w_out": f(inp["ffn1_w_out"]),
        "ffn2_w_in": f(inp["ffn2_w_in"]), "ffn2_w_out": f(inp["ffn2_w_out"]),
        "w_in": f(inp["w_in"]), "mla_q_norm": f(inp["mla_q_norm"]).reshape(DEPTH, 2, 128),
        "mla_w_uq": f(inp["mla_w_uq"]), "mla_kv_norm": f(inp["mla_kv_norm"]).reshape(DEPTH, 1, 128),
        "mla_w_ukv": f(inp["mla_w_ukv"]), "mla_w_o": f(inp["mla_w_o"]),
        "ssm_conv_w": f(inp["ssm_conv_w"]), "ssm_conv_b": f(inp["ssm_conv_b"]).reshape(DEPTH, 8, 128),
        "ssm_small": np.ascontiguousarray(np.concatenate(
            [f(inp["ssm_a_log"]).reshape(DEPTH, 16), f(inp["ssm_dt_bias"]).reshape(DEPTH, 16),
             f(inp["ssm_d"]).reshape(DEPTH, 8)], axis=1).reshape(DEPTH, 1, 40)),
        "ssm_norm": f(inp["ssm_norm"]).reshape(DEPTH, 4, 128), "ssm_w_o": f(inp["ssm_w_o"]),
        "gm_norm": f(inp["gm_norm"]).reshape(DEPTH, 4, 128), "gm_w_s": f(inp["gm_w_s"]),
        "gm_b_s": f(inp["gm_b_s"]), "gm_w_o": f(inp["gm_w_o"]),
        "b_gate": f(inp["b_gate"]).reshape(DEPTH, 24, 128), "w_out": f(inp["w_out"]),
        "final_norm": f(inp["final_norm"]).reshape(8, 128),
        "c_ident": ident, "c_masks": masks, "c_rope": rope, "c_sel": sel,
    }
    x = f(inp["x"])
    c = f(inp["c"])
    ctx = f(inp["ctx"])
    cctx = f(inp["c_ctx"]).reshape(8, 128)
    maps = []
    for b in range(n_cores):
        d = dict(shared)
        d["x"] = x[b]
        d["ctx"] = ctx[b]
        d["crow"] = np.ascontiguousarray(np.concatenate([c[b].reshape(8, 128), cctx], axis=0))
        maps.append(d)
    return maps


_NC_CACHE = {}


def kernel(**inputs):
    if "nc" not in _NC_CACHE:
        _NC_CACHE["nc"] = build_program()
    nc = _NC_CACHE["nc"]
    maps = make_in_maps(inputs, 8)
    res = run_bass_kernel_spmd(nc, maps, core_ids=list(range(8)))
    out = np.stack([np.asarray(res.results[b]["out"], dtype=np.float32) for b in range(8)], axis=0)
    return out
```
